# Optimizing a Trainium2 kernel written in Bass

```python
import jax, jax.numpy as jnp
from jax import lax
import numpy as np

D_MODEL = 1024
BATCH = 8
SEQ = 2048
DEPTH = 2
DEC_BATCH = 32
DEC_SEQ = 4
PAST_LEN = 8192
PAGE_SIZE = 128

HEAD_DIM = 64
HEADS_PER_GROUP = 4
WINDOWS = (128, 512, 2048)
DILATIONS = (1, 4, 16)
N_GROUPS = 3
N_ATT_HEADS = N_GROUPS * HEADS_PER_GROUP
ATT_WIDTH = N_ATT_HEADS * HEAD_DIM
MERGED_ATT_WIDTH = HEADS_PER_GROUP * HEAD_DIM
CHUNK = 128
GMLP_GROUPS = 4
GMLP_WIDTH = D_MODEL // 2
GMLP_GROUP_CH = GMLP_WIDTH // GMLP_GROUPS
D_FF = 2816
IN_WIDTH = 3 * ATT_WIDTH + 2 * GMLP_WIDTH + 2 * D_MODEL
SPLITS = (ATT_WIDTH, 2 * ATT_WIDTH, 3 * ATT_WIDTH,
          3 * ATT_WIDTH + GMLP_WIDTH, 3 * ATT_WIDTH + 2 * GMLP_WIDTH,
          3 * ATT_WIDTH + 2 * GMLP_WIDTH + D_MODEL)
EPS = 1e-6
NEG_INF = -1e30

kernel_name = 'hybrid_dilated_attn_gmlp_decode_step'


def rmsnorm(x, gain):
    xf = x.astype(jnp.float32)
    r = lax.rsqrt(jnp.mean(xf * xf, axis=-1, keepdims=True) + EPS)
    return (xf * r).astype(x.dtype) * gain


def swiglu_half_step(x, norm, w_gate, w_up, w_down):
    h = rmsnorm(x, norm)
    return x + 0.5 * ((jax.nn.silu(h @ w_gate) * (h @ w_up)) @ w_down)


def alibi_slopes():
    h = np.arange(1, N_ATT_HEADS + 1, dtype=np.float32)
    return np.power(np.float32(2.0), -8.0 * h / N_ATT_HEADS).astype(np.float32).reshape(N_GROUPS, HEADS_PER_GROUP)


def dilated_attn_prompt(q, k, v, window, dilation, slopes):
    B, S, H, E = q.shape
    K = window // dilation
    span = dilation * K
    s_pad = -(-S // span) * span
    n = s_pad // dilation
    nb = n // K

    def to_blocks(t):
        t = jnp.pad(t, ((0, 0), (0, s_pad - S), (0, 0), (0, 0)))
        t = t.reshape(B, n, dilation, H, E).transpose(0, 2, 1, 3, 4)
        return t.reshape(B, dilation, nb, K, H, E)

    def with_prev(t):
        prev = jnp.pad(t[:, :, :-1], ((0, 0), (0, 0), (1, 0), (0, 0), (0, 0), (0, 0)))
        return jnp.concatenate([prev, t], axis=3)

    qb = to_blocks(q)
    kb = with_prev(to_blocks(k))
    vb = with_prev(to_blocks(v))
    a = jnp.arange(K)[:, None]
    c = jnp.arange(2 * K)[None, :]
    dist = K + a - c
    blk = jnp.arange(nb)[:, None, None]
    valid = (dist >= 0) & (dist <= K) & ((blk > 0) | (c >= K)[None])
    bias = -slopes[None, :, None] * (dilation * dist)[:, None, :]
    s = jnp.einsum('brnqhe,brnkhe->brnqhk', qb, kb).astype(jnp.float32) * (HEAD_DIM ** -0.5)
    s = jnp.where(valid[None, None, :, :, None, :], s + bias[None, None, None], NEG_INF)
    m = jnp.max(s, axis=-1, keepdims=True)
    p = jnp.exp(s - m)
    l = jnp.sum(p, axis=-1)
    lse = m[..., 0] + jnp.log(l)
    o = jnp.einsum('brnqhk,brnkhe->brnqhe', p.astype(v.dtype), vb) / l[..., None].astype(v.dtype)
    o = o.reshape(B, dilation, n, H, E).transpose(0, 2, 1, 3, 4).reshape(B, s_pad, H, E)[:, :S]
    lse = lse.reshape(B, dilation, n, H).transpose(0, 2, 1, 3).reshape(B, s_pad, H)[:, :S]
    return o, lse


def dilated_attn_sample(q, k_ext, v_ext, past_rows, window, dilation, slopes):
    T = q.shape[1]
    K = window // dilation
    steps = jnp.arange(K + 1) * dilation
    idx = past_rows + jnp.arange(T)[:, None] - steps[None, :]
    valid = idx >= 0
    idx_c = jnp.clip(idx, 0)
    kg = k_ext[:, idx_c]
    vg = v_ext[:, idx_c]
    bias = -slopes[:, None] * steps[None, :]
    s = jnp.einsum('bthe,btkhe->bthk', q, kg).astype(jnp.float32) * (HEAD_DIM ** -0.5)
    s = jnp.where(valid[None, :, None, :], s + bias[None, None], NEG_INF)
    m = jnp.max(s, axis=-1, keepdims=True)
    p = jnp.exp(s - m)
    l = jnp.sum(p, axis=-1)
    lse = m[..., 0] + jnp.log(l)
    o = jnp.einsum('bthk,btkhe->bthe', p.astype(v_ext.dtype), vg) / l[..., None].astype(v_ext.dtype)
    return o, lse


def combine_groups(outs, lses):
    o = jnp.stack(outs, axis=0)
    w = jax.nn.softmax(jnp.stack(lses, axis=0), axis=0)
    return jnp.sum(w[..., None] * o.astype(jnp.float32), axis=0).astype(o.dtype)


def chunk_gmlp(u, vs, w_s, b_s):
    B, T, _ = u.shape
    t_pad = -(-T // CHUNK) * CHUNK
    mask = jnp.tril(jnp.ones((CHUNK, CHUNK), dtype=bool))
    ws = jnp.where(mask[None], w_s, jnp.zeros((), w_s.dtype))
    vb = jnp.pad(vs, ((0, 0), (0, t_pad - T), (0, 0)))
    vb = vb.reshape(B, t_pad // CHUNK, CHUNK, GMLP_GROUPS, GMLP_GROUP_CH)
    z = jnp.einsum('gts,bnsgc->bntgc', ws, vb) + b_s.T[None, None, :, :, None]
    z = z.reshape(B, t_pad, GMLP_WIDTH)[:, :T]
    return u * z


def mixer_inputs(h, w_in, v_norm):
    z = h @ w_in
    q, k, v, u, vs, ga, gb = jnp.split(z, SPLITS, axis=-1)
    shp = h.shape[:-1] + (N_GROUPS, HEADS_PER_GROUP, HEAD_DIM)
    return (q.reshape(shp), k.reshape(shp), v.reshape(shp),
            jax.nn.gelu(u), rmsnorm(jax.nn.gelu(vs), v_norm),
            jax.nn.sigmoid(ga), jax.nn.sigmoid(gb))


def token_mixing(x, mix_norm, w_in, v_norm, w_s, b_s, proj_att, proj_spatial, w_out, past_kv):
    h = rmsnorm(x, mix_norm)
    q, k, v, u, vs, ga, gb = mixer_inputs(h, w_in, v_norm)
    slopes = alibi_slopes()
    outs, lses, new_kv = [], [], []
    for g in range(N_GROUPS):
        qg, kg, vg = q[:, :, g], k[:, :, g], v[:, :, g]
        if past_kv is None:
            o, lse = dilated_attn_prompt(qg, kg, vg, WINDOWS[g], DILATIONS[g], slopes[g])
            keep = min(WINDOWS[g], x.shape[1])
            new_kv.append(jnp.stack([kg[:, -keep:], vg[:, -keep:]], axis=2))
        else:
            buf = past_kv[g]
            k_ext = jnp.concatenate([buf[:, :, 0], kg], axis=1)
            v_ext = jnp.concatenate([buf[:, :, 1], vg], axis=1)
            o, lse = dilated_attn_sample(qg, k_ext, v_ext, buf.shape[1], WINDOWS[g], DILATIONS[g], slopes[g])
            new_kv.append(jnp.stack([kg, vg], axis=2))
        outs.append(o)
        lses.append(lse)
    att = combine_groups(outs, lses)
    att = att.reshape(att.shape[:2] + (MERGED_ATT_WIDTH,))
    spat = chunk_gmlp(u, vs, w_s, b_s)
    out = (ga * (att @ proj_att) + gb * (spat @ proj_spatial)) @ w_out
    return x + out, new_kv, vs


def setup_inputs(seed: int = 0) -> dict:
    key = jax.random.key(seed)
    ks = iter(jax.random.split(key, 32))

    def nrm(shape, scale):
        return scale * jax.random.normal(next(ks), shape, jnp.float32)

    def gain(shape):
        return 1.0 + 0.01 * jax.random.normal(next(ks), shape, jnp.float32)

    inputs = {
        'x_prompt': nrm((BATCH, SEQ, D_MODEL), 1.0),
        'x_sample': nrm((DEC_BATCH, DEC_SEQ, D_MODEL), 1.0),
    }
    for w in WINDOWS:
        rows = min(w, PAST_LEN)
        inputs['cache_kv_w%d' % w] = nrm((DEPTH, DEC_BATCH, rows, 2, HEADS_PER_GROUP, HEAD_DIM), 1.0)
    inputs['ffn1_norm'] = gain((DEPTH, D_MODEL))
    inputs['ffn1_gate'] = nrm((DEPTH, D_MODEL, D_FF), D_MODEL ** -0.5)
    inputs['ffn1_up'] = nrm((DEPTH, D_MODEL, D_FF), D_MODEL ** -0.5)
    inputs['ffn1_down'] = nrm((DEPTH, D_FF, D_MODEL), D_FF ** -0.5)
    inputs['mix_norm'] = gain((DEPTH, D_MODEL))
    inputs['w_in'] = nrm((DEPTH, D_MODEL, IN_WIDTH), D_MODEL ** -0.5)
    inputs['gmlp_v_norm'] = gain((DEPTH, GMLP_WIDTH))
    inputs['gmlp_ws'] = nrm((DEPTH, GMLP_GROUPS, CHUNK, CHUNK), CHUNK ** -0.5)
    inputs['gmlp_bias'] = 1.0 + nrm((DEPTH, GMLP_GROUPS, CHUNK), 0.1)
    inputs['proj_att'] = nrm((DEPTH, MERGED_ATT_WIDTH, D_MODEL), MERGED_ATT_WIDTH ** -0.5)
    inputs['proj_spatial'] = nrm((DEPTH, GMLP_WIDTH, D_MODEL), GMLP_WIDTH ** -0.5)
    inputs['w_out'] = nrm((DEPTH, D_MODEL, D_MODEL), D_MODEL ** -0.5)
    inputs['ffn2_norm'] = gain((DEPTH, D_MODEL))
    inputs['ffn2_gate'] = nrm((DEPTH, D_MODEL, D_FF), D_MODEL ** -0.5)
    inputs['ffn2_up'] = nrm((DEPTH, D_MODEL, D_FF), D_MODEL ** -0.5)
    inputs['ffn2_down'] = nrm((DEPTH, D_FF, D_MODEL), D_FF ** -0.5)
    inputs['final_norm'] = gain((D_MODEL,))
    return inputs


def reference(x_prompt, x_sample, cache_kv_w128, cache_kv_w512, cache_kv_w2048,
              ffn1_norm, ffn1_gate, ffn1_up, ffn1_down, mix_norm, w_in, gmlp_v_norm,
              gmlp_ws, gmlp_bias, proj_att, proj_spatial, w_out,
              ffn2_norm, ffn2_gate, ffn2_up, ffn2_down, final_norm):
    caches = (cache_kv_w128, cache_kv_w512, cache_kv_w2048)
    xp, xs = x_prompt, x_sample
    kv_p = [[] for _ in range(N_GROUPS)]
    kv_s = [[] for _ in range(N_GROUPS)]
    gv_s = []
    for l in range(DEPTH):
        mix_w = (mix_norm[l], w_in[l], gmlp_v_norm[l], gmlp_ws[l], gmlp_bias[l],
                 proj_att[l], proj_spatial[l], w_out[l])
        xp = swiglu_half_step(xp, ffn1_norm[l], ffn1_gate[l], ffn1_up[l], ffn1_down[l])
        xp, new_p, _ = token_mixing(xp, *mix_w, None)
        xp = swiglu_half_step(xp, ffn2_norm[l], ffn2_gate[l], ffn2_up[l], ffn2_down[l])
        xs = swiglu_half_step(xs, ffn1_norm[l], ffn1_gate[l], ffn1_up[l], ffn1_down[l])
        xs, new_s, vs_s = token_mixing(xs, *mix_w, (caches[0][l], caches[1][l], caches[2][l]))
        xs = swiglu_half_step(xs, ffn2_norm[l], ffn2_gate[l], ffn2_up[l], ffn2_down[l])
        for g in range(N_GROUPS):
            kv_p[g].append(new_p[g])
            kv_s[g].append(new_s[g])
        gv_s.append(vs_s)
    y_prompt = rmsnorm(xp, final_norm)
    y_sample = rmsnorm(xs, final_norm)
    kv_w128_prompt = jnp.stack(kv_p[0], axis=0)
    kv_w512_prompt = jnp.stack(kv_p[1], axis=0)
    kv_w2048_prompt = jnp.stack(kv_p[2], axis=0)
    kv_w128_sample = jnp.stack(kv_s[0], axis=0)
    kv_w512_sample = jnp.stack(kv_s[1], axis=0)
    kv_w2048_sample = jnp.stack(kv_s[2], axis=0)
    gmlp_v_sample = jnp.stack(gv_s, axis=0)
    return (y_prompt, y_sample, kv_w128_prompt, kv_w512_prompt, kv_w2048_prompt,
            kv_w128_sample, kv_w512_sample, kv_w2048_sample, gmlp_v_sample)
```

```python
import bisect
import os
from contextlib import ExitStack

import numpy as np
import concourse.bass as bass
import concourse.mybir as mybir
from concourse.bass_utils import run_bass_kernel_spmd

F32 = mybir.dt.float32
BF16 = mybir.dt.bfloat16
AF = mybir.ActivationFunctionType
ALU = mybir.AluOpType
AX = mybir.AxisListType

NCORES = 8
D = 1024
KC = 8
S = 2048
NS = 16
NT = S + NS
DFF = 2816
NJ = 22
INW = 5376
DEPTH = 2
WIN = (128, 512, 2048)
DIL = (1, 4, 16)
EPS = 1e-6
NEG = -30000.0
TT = [(0, 400), (400, 400), (800, 400), (1200, 400), (1600, 464)]
LT = [(0, 512), (512, 512), (1024, 512), (1536, 512), (2048, 16)]


def tov(lo, hi):
    return [t for t, (t0, n) in enumerate(TT) if t0 < hi and t0 + n > lo]

ENGINES = ("pe", "act", "dve", "pool", "sp")
SAME_ENGINE_SYNC = bool(int(os.environ.get("MK_SES", "1")))
CAP = int(os.environ.get("MK_CAP", "500"))


class Region:
    __slots__ = ("name", "last_w", "readers", "sem", "dma_total", "excl")

    def __init__(self, name, excl=False):
        self.name = name
        self.excl = excl
        self.last_w = None
        self.readers = []
        self.sem = None
        self.dma_total = 0


class Planner:
    def __init__(self, nc):
        self.nc = nc
        self.ops = {e: [] for e in ENGINES}
        self.dma_sems = []

    def _deps(self, eng, reads, writes, is_dma):
        deps = []
        for r in reads:
            if r.last_w is not None:
                deps.append(r.last_w)
        for w in writes:
            if w.last_w is not None:
                deps.append(w.last_w)
            deps.extend(w.readers)
        out = []
        seen = set()
        for d in deps:
            key = (d[0], id(d[1]) if d[0] == "dma" else d[1], d[2])
            if key in seen:
                continue
            seen.add(key)
            if d[0] == "eng":
                if d[1] == eng and not is_dma and (eng == "pe" or not SAME_ENGINE_SYNC):
                    continue
                self.ops[d[1]][d[2]]["signal"] = True
            out.append(d)
        mx = {}
        for d in out:
            if d[0] == "dma":
                mx[id(d[1])] = max(mx.get(id(d[1]), 0), d[2])
        out = [d for d in out if d[0] != "dma" or d[2] == mx[id(d[1])]]
        return out

    def op(self, eng, fn, reads=(), writes=()):
        ex = [r for r in reads if r.excl]
        if ex:
            writes = list(writes) + [r for r in ex if r not in writes]
            reads = [r for r in reads if not r.excl]
        deps = self._deps(eng, reads, writes, False)
        seq = len(self.ops[eng])
        self.ops[eng].append(dict(kind="op", fn=fn, deps=deps, signal=False))
        me = ("eng", eng, seq)
        for r in reads:
            r.readers = [x for x in r.readers if not (x[0] == "eng" and x[1] == eng)]
            r.readers.append(me)
        for w in writes:
            w.last_w = me
            w.readers = []

    def dma(self, eng, fn, owner, reads=(), writes=()):
        deps = self._deps(eng, reads, writes, True)
        if owner.sem is None or owner.dma_total >= 480:
            h = Region(owner.name)
            h.sem = "pending"
            self.dma_sems.append(h)
            owner.sem = h
            owner.dma_total = 0
        owner.dma_total += 16
        holder = owner.sem
        self.ops[eng].append(dict(kind="dma", fn=fn, deps=deps, signal=False, owner=holder))
        me = ("dma", holder, owner.dma_total)
        for r in reads:
            r.readers.append(me)
        for w in writes:
            w.last_w = me
            w.readers = []

    def finish(self, eng, regions):
        deps = self._deps(eng, regions, regions, True)
        self.ops[eng].append(dict(kind="nop", fn=None, deps=deps, signal=False))

    def emit(self, stack):
        nc = self.nc
        sig_seq = {}
        for e in ENGINES:
            sig_seq[e] = [i for i, r in enumerate(self.ops[e]) if r["signal"]]
        eng_sems = {}
        for e in ENGINES:
            n = (len(sig_seq[e]) + CAP - 1) // CAP
            eng_sems[e] = [stack.enter_context(nc.semaphore("s_%s_%d" % (e, i))) for i in range(n)]
        for i_, rg in enumerate(self.dma_sems):
            rg.sem = stack.enter_context(nc.semaphore("d%d_%s" % (i_, rg.name)))
        print("semaphores:", len(self.dma_sems), "dma +", sum(len(v) for v in eng_sems.values()), "engine")

        def sig_index(e, seq):
            i = bisect.bisect_left(sig_seq[e], seq)
            assert i < len(sig_seq[e]) and sig_seq[e][i] == seq, (e, seq)
            return i

        eng_obj = {"pe": "tensor", "act": "scalar", "dve": "vector", "pool": "gpsimd", "sp": "sync"}
        stats = {}

        def run_engine(e, engine):
            seen_eng = {x: -1 for x in ENGINES}
            seen_dma = {}
            nwait = 0
            my_sig = 0
            for rec in self.ops[e]:
                for d in rec["deps"]:
                    if d[0] == "eng":
                        k = sig_index(d[1], d[2])
                        if k <= seen_eng[d[1]]:
                            continue
                        seen_eng[d[1]] = k
                        engine.wait_ge(eng_sems[d[1]][k // CAP], k % CAP + 1)
                        nwait += 1
                    else:
                        rg, val = d[1], d[2]
                        if seen_dma.get(id(rg), 0) >= val:
                            continue
                        seen_dma[id(rg)] = val
                        engine.wait_ge(rg.sem, val)
                        nwait += 1
                if rec["kind"] == "nop":
                    continue
                ins = rec["fn"](engine)
                if rec["kind"] == "dma":
                    ins.then_inc(rec["owner"].sem, 16)
                if rec["signal"]:
                    k = my_sig
                    my_sig += 1
                    ins.then_inc(eng_sems[e][k // CAP], 1)
            stats[e] = (len(self.ops[e]), nwait, my_sig)

        with nc.Block() as block:
            for e in ENGINES:
                if self.ops[e]:
                    getattr(block, eng_obj[e])(lambda engine, e=e: run_engine(e, engine))
        return stats


def alibi_slopes():
    h = np.arange(1, 13, dtype=np.float32)
    return np.power(np.float32(2.0), -8.0 * h / 12).astype(np.float32).reshape(3, 4)


C_IDENT = 0
C_TRIU = 128
C_BIASC = 256
C_BIASN = C_BIASC + 48
C_MASKS = C_BIASN + 192
C_RES = C_MASKS + 64
C_SEL = C_RES
C_BIASP = C_SEL + 2048
C_TOT = C_BIASP + 12 * 256


def build_consts():
    c = np.zeros((128, C_TOT), np.float32)
    c[:, C_IDENT:C_IDENT + 128] = np.eye(128, dtype=np.float32)
    s_i = np.arange(128)[:, None]
    t_i = np.arange(128)[None, :]
    c[:, C_TRIU:C_TRIU + 128] = (s_i <= t_i).astype(np.float32)
    sl = alibi_slopes()
    kk = np.arange(128)[:, None].astype(np.float32)
    aa = np.arange(128)[None, :].astype(np.float32)
    for g in range(3):
        for h in range(4):
            m = sl[g, h] * DIL[g]
            diag = np.where(aa >= kk, -m * (aa - kk), NEG)
            prev = np.where(kk >= aa, -m * (128.0 + aa - kk), NEG)
            o = C_BIASP + (g * 4 + h) * 256
            c[:, o:o + 128] = diag
            c[:, o + 128:o + 256] = prev
    n = np.arange(128).astype(np.float32)
    for g in range(3):
        for j in range(4):
            for h in range(4):
                col = C_BIASC + g * 16 + j * 4 + h
                if g == 0:
                    c[:, col] = np.where(n >= j, -sl[0, h] * (128.0 + j - n), NEG)
                else:
                    c[:, col] = -sl[g, h] * DIL[g] * (128.0 - n)
    for g in range(3):
        for h in range(4):
            for q in range(16):
                for k in range(16):
                    col = C_BIASN + g * 64 + h * 16 + q
                    i, j = divmod(q, 4)
                    i2, j2 = divmod(k, 4)
                    if i != i2:
                        v = NEG
                    elif g == 0:
                        v = -sl[0, h] * (j - j2) if j2 <= j else NEG
                    else:
                        v = 0.0 if j2 == j else NEG
                    c[k, col] = v
    for g in range(4):
        for k in range(16):
            for t in range(16):
                i, s = divmod(k, 4)
                i2, t2 = divmod(t, 4)
                c[k, C_MASKS + g * 16 + t] = 1.0 if (i == i2 and s <= t2) else 0.0
    for j in range(16):
        c[j, C_SEL + j * 128:C_SEL + (j + 1) * 128] = 1.0
    return c


def build_program():
    nc = bass.Bass("TRN2", target_bir_lowering=False)

    def din(name, shape):
        return nc.dram_tensor(name, list(shape), F32, kind="ExternalInput").ap()

    def dout(name, shape):
        return nc.dram_tensor(name, list(shape), F32, kind="ExternalOutput").ap()

    xp = din("xp", [S, D])
    xs = din("xs", [NS, D])
    c128 = din("c128", [DEPTH, 4, 128, 512])
    c512 = din("c512", [DEPTH, 4, 512, 512])
    c2048 = din("c2048", [DEPTH, 4, 2048, 512])
    caches = (c128, c512, c2048)
    gains_d = din("gains", [128, 56])
    consts_d = din("consts", [128, C_TOT])
    w_gate = (din("ffn1_gate", [DEPTH, D, DFF]), din("ffn2_gate", [DEPTH, D, DFF]))
    w_up = (din("ffn1_up", [DEPTH, D, DFF]), din("ffn2_up", [DEPTH, D, DFF]))
    w_down = (din("ffn1_down", [DEPTH, DFF, D]), din("ffn2_down", [DEPTH, DFF, D]))
    w_in = din("w_in", [DEPTH, D, INW])
    vnorm_d = din("gmlp_v_norm", [DEPTH, 512])
    ws_d = din("gmlp_ws", [DEPTH, 4, 128, 128])
    gb_d = din("gmlp_bias", [DEPTH, 512])
    patt_d = din("proj_att", [DEPTH, 256, D])
    psp_d = din("proj_spatial", [DEPTH, 512, D])
    wout_d = din("w_out", [DEPTH, D, D])

    yp = dout("yp", [S, D])
    ys = dout("ys", [NS, D])
    kvp = (dout("kvp128", [DEPTH, 128, 512]), dout("kvp512", [DEPTH, 512, 512]),
           dout("kvp2048", [DEPTH, 2048, 512]))
    kvs = dout("kvs", [3, DEPTH, NS, 512])
    gvs = dout("gvs", [DEPTH, NS, 512])

    with ExitStack() as st:
        ARENA_W = 53100
        arena = st.enter_context(nc.sbuf_tensor("arena", [128, ARENA_W], F32))
        banks = [st.enter_context(nc.psum_tensor("bank%d" % i, [128, 512], F32)) for i in range(8)]
        BANK = [Region("bank%d" % i, excl=True) for i in range(8)]
        P = Planner(nc)

        cur = [0]

        def take(nwords):
            o = cur[0]
            cur[0] += (nwords + 1) // 2 * 2
            assert cur[0] <= ARENA_W, cur[0]
            return o

        def fv(off, n):
            return arena[:, off:off + n]

        def bv(off, n):
            v = arena[:, off:off + n // 2].bitcast(BF16)
            assert tuple(v.shape) == (128, n), v.shape
            return v

        o_x = take(KC * NT)
        xT = fv(o_x, KC * NT).rearrange("p (k n) -> p k n", k=KC)
        o_h = take(KC * NT // 2)
        hT = bv(o_h, KC * NT).rearrange("p (k n) -> p k n", k=KC)
        o_c = take(C_RES)
        cst = fv(o_c, C_RES)
        o_g = take(56)
        gains = fv(o_g, 56)
        o_mh = take(512)
        mhalf = fv(o_mh, 512)
        o_ob = take(64)
        ones_bf = bv(o_ob, 128)
        o_of = take(128)
        ones_f = fv(o_of, 128)
        o_ws = take(256)
        wsT = bv(o_ws, 512).rearrange("p (g t) -> p g t", g=4)
        o_wss = take(32)
        wsS = bv(o_wss, 64).rearrange("p (g t) -> p g t", g=4)
        o_wsf = take(64)
        wsSf = fv(o_wsf, 64)
        o_sm = take(64)
        small = fv(o_sm, 64)
        NWP = 5
        o_wp = [take(1024) for _ in range(NWP)]
        wp = [bv(o, 2048) for o in o_wp]
        WP = [Region("wp%d" % i) for i in range(NWP)]
        R_cst, R_gains, R_ones, R_mh = (Region(n) for n in ("cst", "gains", "ones", "mh"))
        R_wsT, R_wsS, R_wsSf, R_small = (Region(n) for n in ("wsT", "wsS", "wsSf", "small"))
        OUTR = []
        X = [[Region("x%d_%d" % (k, t)) for t in range(5)] for k in range(KC)]
        H = [Region("h%d" % t) for t in range(5)]
        arena_base = cur[0]

        ident = cst[:, C_IDENT:C_IDENT + 128]
        triu = cst[:, C_TRIU:C_TRIU + 128]

        bank_i = [0]

        reserved = set()

        def nb(reserve=False):
            i = bank_i[0]
            while i in reserved:
                i = (i + 1) % 8
            bank_i[0] = (i + 1) % 8
            if reserve:
                reserved.add(i)
            return banks[i], BANK[i]

        wp_i = [0]

        def wload(view_fn, src, eng="pool"):
            i = wp_i[0]
            wp_i[0] = (i + 1) % NWP
            dst = view_fn(wp[i])
            P.dma(eng, lambda e: e.dma_start(out=dst, in_=src), WP[i], writes=[WP[i]])
            return wp[i], WP[i]

        def mm(out, lhsT, rhs, start, stop, reads, writes):
            P.op("pe", lambda e: e.matmul(out, lhsT=lhsT, rhs=rhs, start=start, stop=stop,
                                          skip_group_check=True), reads, writes)

        def act(out, in_, func, reads, writes, scale=None, bias=None, accum=None):
            kw = {}
            if scale is not None:
                kw["scale"] = scale
            if bias is not None:
                kw["bias"] = bias
            if accum is not None:
                kw["accum_out"] = accum
            P.op("act", lambda e: e.activation(out=out, in_=in_, func=func, **kw), reads, writes)

        def tt_op(eng, out, in0, in1, op, reads, writes):
            P.op(eng, lambda e: e.tensor_tensor(out=out, in0=in0, in1=in1, op=op), reads, writes)

        def ts_op(eng, out, in0, s1, op0, reads, writes, s2=None, op1=None):
            if op1 is None:
                P.op(eng, lambda e: e.tensor_scalar(out=out, in0=in0, scalar1=s1, scalar2=None, op0=op0), reads, writes)
            else:
                P.op(eng, lambda e: e.tensor_scalar(out=out, in0=in0, scalar1=s1, scalar2=s2, op0=op0, op1=op1), reads, writes)

        def stt(out, in0, scalar, in1, op0, op1, reads, writes):
            P.op("dve", lambda e: e.scalar_tensor_tensor(out=out, in0=in0, scalar=scalar, in1=in1, op0=op0, op1=op1),
                 reads, writes)

        def cp(eng, out, in_, reads, writes):
            if eng == "act":
                P.op("act", lambda e: e.copy(out=out, in_=in_), reads, writes)
            else:
                P.op(eng, lambda e: e.tensor_copy(out=out, in_=in_), reads, writes)

        def dma(eng, out, in_, owner, reads=(), writes=(), slow=False):
            if slow:
                P.dma(eng, lambda e: e.dma_start(out=out, in_=in_, allow_slow_non_contiguous=True), owner, reads, writes)
            else:
                P.dma(eng, lambda e: e.dma_start(out=out, in_=in_), owner, reads, writes)

        def alias(new_regions, old_regions):
            acc = []
            for o in old_regions:
                if o.last_w is not None:
                    acc.append(o.last_w)
                acc.extend(o.readers)
            for n in new_regions:
                n.last_w = None
                n.readers = list(acc) + n.readers

        phase_regs = []

        def new_phase(regs):
            alias(regs, phase_regs)
            del phase_regs[:]
            phase_regs.extend(regs)

        dma("sp", cst, consts_d[:, 0:C_RES], R_cst, writes=[R_cst])
        P.op("dve", lambda e: e.memset(mhalf, -0.5), writes=[R_mh])
        dma("sp", gains, gains_d, R_gains, writes=[R_gains])
        P.op("dve", lambda e: e.memset(ones_bf, 1.0), writes=[R_ones])
        P.op("dve", lambda e: e.memset(ones_f, 1.0), writes=[R_ones])

        cur[0] = arena_base
        o_stg = take(2 * 4 * 1024)
        xstgs = [fv(o_stg + 4096 * i_, 4096).rearrange("p (b f) -> p b f", b=4) for i_ in range(2)]
        R_xstgs = [Region("xstg%d" % i_) for i_ in range(2)]
        new_phase(R_xstgs)
        for t in range(5):
            xstg, R_xstg = xstgs[t % 2], R_xstgs[t % 2]
            t0, n = LT[t]
            if t < 4:
                dma("sp", xstg, xp[t0:t0 + 512, :].rearrange("(b p) f -> p b f", p=128), R_xstg, writes=[R_xstg])
            else:
                dma("sp", xstg[0:16, 0, :], xs, R_xstg, writes=[R_xstg])
            for k in range(KC):
                bk, BK = nb()
                if t < 4:
                    for b in range(4):
                        P.op("pe", lambda e, bk=bk, b=b, k=k, xstg=xstg: e.transpose(bk[:, b * 128:(b + 1) * 128], xstg[:, b, k * 128:(k + 1) * 128], ident),
                             [R_xstg, R_cst], [BK])
                else:
                    P.op("pe", lambda e, bk=bk, k=k, xstg=xstg: e.transpose(bk[:, 0:16], xstg[0:16, 0, k * 128:(k + 1) * 128], ident[0:16, 0:16]),
                         [R_xstg, R_cst], [BK])
                cp("act" if k % 2 else "dve", xT[:, k, t0:t0 + n], bk[:, 0:n], [BK], [X[k][t_] for t_ in tov(t0, t0 + n)])

        def rmsnorm(gi):
            o0 = cur[0]
            o_sq = take(4 * 256)
            sq = [bv(o_sq + 256 * i, 512) for i in range(4)]
            R_sq = [Region("sq%d" % i) for i in range(4)]
            o_rs = take(512)
            rs = fv(o_rs, 512)
            R_rs = Region("rs")
            new_regs = R_sq + [R_rs]
            alias(new_regs, phase_regs)
            phase_regs.extend(new_regs)
            for t in range(5):
                t0, n = TT[t]
                bk, BK = nb()
                for k in range(KC):
                    i = k % 4
                    if k % 2 == 0:
                        act(sq[i][:, 0:n], xT[:, k, t0:t0 + n], AF.Square, [X[k][t]], [R_sq[i]])
                    else:
                        tt_op("pool", sq[i][:, 0:n], xT[:, k, t0:t0 + n], xT[:, k, t0:t0 + n], ALU.mult, [X[k][t]], [R_sq[i]])
                    mm(bk[:, 0:n], ones_bf, sq[i][:, 0:n], k == 0, k == KC - 1, [R_sq[i], R_ones], [BK])
                act(rs[:, 0:n], bk[:, 0:n], AF.Ln, [BK], [R_rs], scale=1.0 / D, bias=EPS)
                act(bk[:, 0:n], rs[:, 0:n], AF.Exp, [R_rs], [BK], scale=-0.5)
                for k in range(KC):
                    stt(hT[:, k, t0:t0 + n], xT[:, k, t0:t0 + n], gains[:, gi * 8 + k:gi * 8 + k + 1], bk[:, 0:n],
                        ALU.mult, ALU.mult, [X[k][t], R_gains, BK], [H[t]])
            cur[0] = o0

        def ffn(l, which):
            gi = l * 3 + (0 if which == 0 else 2)
            cur[0] = arena_base
            HC = 6
            o_hid = take(HC * NT // 2)
            hid = bv(o_hid, HC * NT).rearrange("p (j n) -> p j n", j=HC)
            HID = [[Region("hid%d_%d" % (j, t)) for t in range(5)] for j in range(HC)]
            o_sg = take(2 * 512)
            sg = [fv(o_sg + 512 * i, 512) for i in range(2)]
            R_sg = [Region("sg%d" % i) for i in range(2)]
            new_phase([r for row in HID for r in row] + R_sg)
            Wg, Wu, Wd = w_gate[which][l], w_up[which][l], w_down[which][l]
            vfn0 = lambda b: b.rearrange("p (k n) -> p k n", k=KC)
            pre_gu = {0: (wload(vfn0, Wg[:, 0:256].rearrange("(k p) n -> p k n", p=128)),
                          wload(vfn0, Wu[:, 0:256].rearrange("(k p) n -> p k n", p=128)))}
            rmsnorm(gi)
            sgi = 0
            for j0 in range(0, NJ, HC):
                nj = min(HC, NJ - j0)
                for s0 in range(0, nj, 2):
                    ja = j0 + s0
                    vfn = lambda b: b.rearrange("p (k n) -> p k n", k=KC)
                    if ja in pre_gu:
                        (wg, RG), (wu, RU) = pre_gu.pop(ja)
                    else:
                        wg, RG = wload(vfn, Wg[:, ja * 128:(ja + 2) * 128].rearrange("(k p) n -> p k n", p=128))
                        wu, RU = wload(vfn, Wu[:, ja * 128:(ja + 2) * 128].rearrange("(k p) n -> p k n", p=128))
                    wg3, wu3 = vfn(wg), vfn(wu)
                    for jj in range(2):
                        jl = s0 + jj
                        for t in range(5):
                            t0, n = TT[t]
                            bg, BG = nb()
                            for k in range(KC):
                                mm(bg[:, 0:n], wg3[:, k, jj * 128:(jj + 1) * 128], hT[:, k, t0:t0 + n], k == 0, k == KC - 1, [RG, H[t]], [BG])
                            bu, BU = nb()
                            for k in range(KC):
                                mm(bu[:, 0:n], wu3[:, k, jj * 128:(jj + 1) * 128], hT[:, k, t0:t0 + n], k == 0, k == KC - 1, [RU, H[t]], [BU])
                            i = sgi % 2
                            sgi += 1
                            act(sg[i][:, 0:n], bg[:, 0:n], AF.Silu, [BG], [R_sg[i]])
                            tt_op("dve", hid[:, jl, t0:t0 + n], sg[i][:, 0:n], bu[:, 0:n], ALU.mult, [R_sg[i], BU], [HID[jl][t]])
                wds = []
                for s0 in range(0, nj, 2):
                    ja = j0 + s0
                    vfn = lambda b: b.rearrange("p (j n) -> p j n", j=2)
                    wd, RD = wload(vfn, Wd[ja * 128:(ja + 2) * 128, :].rearrange("(j p) n -> p j n", p=128))
                    wds.append((vfn(wd), RD))
                last_part = (j0 + HC >= NJ)
                order = [(m, t) for t in range(5) for m in range(KC)] if last_part else [(m, t) for m in range(KC) for t in range(5)]
                for m, t in order:
                    if True:
                        t0, n = TT[t]
                        bk, BK = nb()
                        for jl in range(nj):
                            wd3, RD = wds[jl // 2]
                            mm(bk[:, 0:n], wd3[:, jl % 2, m * 128:(m + 1) * 128], hid[:, jl, t0:t0 + n], jl == 0, jl == nj - 1,
                               [RD, HID[jl][t]], [BK])
                        stt(xT[:, m, t0:t0 + n], bk[:, 0:n], 0.5, xT[:, m, t0:t0 + n], ALU.mult, ALU.add, [BK, X[m][t]], [X[m][t]])

        def mixer(l):
            cur[0] = arena_base
            o_att = take(2 * NT // 2)
            attT = bv(o_att, 2 * NT).rearrange("p (c n) -> p c n", c=2)
            AT = [[Region("att%d_%d" % (c, t)) for t in range(5)] for c in range(2)]
            o_qts = take(6 * 2 * 16 // 2)
            qTs = bv(o_qts, 192).rearrange("p (c h n) -> p c h n", c=6, h=2)
            o_kts = take(6 * 16 // 2)
            kTs = bv(o_kts, 96).rearrange("p (c n) -> p c n", c=6)
            o_qtok = take(768 // 2)
            qtok = bv(o_qtok, 768).rearrange("p (g c f) -> p g c f", g=3, c=2)
            o_kvs = take(3 * 512)
            kv_s = fv(o_kvs, 1536).rearrange("p (g kv f) -> p g kv f", g=3, kv=2)
            R_qTs, R_kTs, R_qtok, R_kvs = Region("qTs"), Region("kTs"), Region("qtok"), Region("kvs")
            o_vss = take(512)
            vs_s = fv(o_vss, 512)
            R_vss = Region("vs_s")
            mixer_base = cur[0]
            persist = [r for row in AT for r in row] + [R_qTs, R_kTs, R_qtok, R_kvs, R_vss]
            new_phase(persist)
            OUTR.extend([R_kvs, R_vss])
            P.op("dve", lambda e: e.memset(qTs, 0.0), writes=[R_qTs])

            def proj_loads(c, g):
                qc0 = g * 256 + c * 128
                v1 = lambda b: b[:, 0:1024].rearrange("p (k n) -> p k n", k=KC)
                v2 = lambda b: b.rearrange("p (k n) -> p k n", k=KC)
                wa, RA = wload(v1, w_in[l][:, qc0:qc0 + 128].rearrange("(k p) n -> p k n", p=128))
                wq3 = v1(wa)
                i = wp_i[0]
                wp_i[0] = (i + 1) % NWP
                wb3 = v2(wp[i])
                RB = WP[i]
                dma("pool", wb3[:, :, 0:128], w_in[l][:, 768 + qc0:768 + qc0 + 128].rearrange("(k p) n -> p k n", p=128), RB, writes=[RB])
                dma("pool", wb3[:, :, 128:256], w_in[l][:, 1536 + qc0:1536 + qc0 + 128].rearrange("(k p) n -> p k n", p=128), RB, writes=[RB])
                return wq3, RA, wb3, RB

            pre0 = proj_loads(0, 0)
            rmsnorm(l * 3 + 1)
            sub = int(os.environ.get("MK_SUB", "99"))
            if sub < 2:
                return

            o_bp = cur[0]
            BS = []
            for si_ in range(2):
                b_ = {}
                ob_ = take(512)
                b_["biasp"] = fv(ob_, 512)
                b_["R_bp"] = Region("bp%d" % si_)
                oq_ = take(NT // 2)
                b_["qT"] = bv(oq_, NT)
                ok_ = take(NT // 2)
                b_["kT"] = bv(ok_, NT)
                ov_ = take(16 * 2 * 66 // 2)
                b_["vaug"] = bv(ov_, 16 * 132).rearrange("p (b h e) -> p b h e", b=16, h=2)
                b_["R_q"], b_["R_k"] = Region("qT%d" % si_), Region("kT%d" % si_)
                b_["VA"] = [Region("va%d_%d" % (si_, b)) for b in range(16)]
                BS.append(b_)
            o_oa = take(2 * NT)
            oacc = fv(o_oa, 2 * NT).rearrange("p (h n) -> p h n", h=2)
            o_pt = take(4 * 256)
            ptb = [bv(o_pt + 256 * i, 512) for i in range(4)]
            o_st = take(2 * 256)
            stg = [fv(o_st + 256 * i, 256) for i in range(2)]
            o_rl = take(NT)
            rl = fv(o_rl, NT)
            o_ssb = take(2 * 512)
            ssb = [fv(o_ssb + 512 * i, 512) for i in range(2)]
            SSB = [Region("ssb%d" % i) for i in range(2)]
            R_vones = Region("vones")
            OA = [Region("oa%d" % h) for h in range(2)]
            PTB = [Region("pt%d" % i) for i in range(4)]
            STG = [Region("stg%d" % i) for i in range(2)]
            R_rl = Region("rl")
            RL4 = [Region("rl%d" % i) for i in range(4)]
            att_regs = [R_vones, R_rl] + RL4 + OA + PTB + STG + SSB
            for b_ in BS:
                att_regs += [b_["R_q"], b_["R_k"], b_["R_bp"]] + b_["VA"]
            OUTR.extend(STG)
            alias(att_regs, phase_regs[len(persist):])
            del phase_regs[len(persist):]
            phase_regs.extend(att_regs)
            for b_ in BS:
                P.op("dve", lambda e, b_=b_: e.memset(b_["vaug"][:, :, :, 64:65], 1.0), writes=[R_vones] + b_["VA"])
            pti = [0]
            sti = [0]

            def gen_proj(c, g, B, pre=None):
                biasp, R_bp, qT, kT, vaug, R_q, R_k, VA = (B[x] for x in ("biasp", "R_bp", "qT", "kT", "vaug", "R_q", "R_k", "VA"))
                d = DIL[g]
                nper = S // d
                nbr = nper // 128
                wq3, RA, wb3, RB = pre if pre is not None else proj_loads(c, g)
                dma("sp", biasp.rearrange("p (h f) -> p h f", h=2),
                    consts_d[:, C_BIASP + (g * 4 + 2 * c) * 256:C_BIASP + (g * 4 + 2 * c + 2) * 256].rearrange("p (h f) -> p h f", h=2),
                    R_bp, writes=[R_bp])
                for which, w3, RW, dstT, RT, dsts, RS_, scale in ((0, wq3, RA, qT, R_q, qTs, R_qTs, 0.125),
                                                                 (1, wb3, RB, kT, R_k, kTs, R_kTs, 1.0)):
                    for t in range(5):
                        t0, n = TT[t]
                        bk, BK = nb()
                        for k in range(KC):
                            mm(bk[:, 0:n], w3[:, k, 0:128], hT[:, k, t0:t0 + n], k == 0, k == KC - 1, [RW, H[t]], [BK])
                        npr = min(t0 + n, S) - t0
                        dst = dstT[:, t0:t0 + npr]
                        src = bk[:, 0:npr]
                        act(dst, src, AF.Copy, [BK], [RT], scale=scale)
                        if npr < n:
                            if which == 0:
                                act(dsts[0:64, g * 2 + c, 0, :], bk[0:64, npr:n], AF.Copy, [BK], [RS_], scale=scale)
                                act(dsts[64:128, g * 2 + c, 1, :], bk[64:128, npr:n], AF.Copy, [BK], [RS_], scale=scale)
                            else:
                                act(dsts[:, g * 2 + c, :], bk[:, npr:n], AF.Copy, [BK], [RS_], scale=scale)
                        yield
                for blk in range(16):
                    r = blk // nbr
                    n0 = (blk % nbr) * 128
                    lo = r + n0 * d
                    hi = r + (n0 + 127) * d + 1
                    need_out = (g == 2) or (g == 1 and blk % nbr == nbr - 1) or (g == 0 and blk == 15)
                    hregs = [H[t] for t in tov(lo, hi)]
                    bk, BK = nb()
                    c0 = 0 if need_out else 128
                    for k in range(KC):
                        mm(bk[:, c0:256], hT[:, k, lo:hi:d], wb3[:, k, c0:256], k == 0, k == KC - 1, [RB] + hregs, [BK])
                    cp("act", vaug[:, blk, :, 0:64], bk[:, 128:256].rearrange("p (h e) -> p h e", h=2), [BK], [VA[blk]])
                    if need_out:
                        si = sti[0] % 2
                        sti[0] += 1
                        cp("dve", stg[si], bk[:, 0:256], [BK], [STG[si]])
                        row0 = lo - (S - WIN[g])
                        dst = kvp[g][l].rearrange("t (kv f) -> t kv f", kv=2)[row0:row0 + 127 * d + 1:d, :, c * 128:(c + 1) * 128]
                        dma("sp", dst, stg[si].rearrange("p (kv f) -> p kv f", kv=2), STG[si], reads=[STG[si]])
                    yield
                bk, BK = nb()
                for k in range(KC):
                    mm(bk[0:16, 0:256], hT[:, k, S:NT], wb3[:, k, :], k == 0, k == KC - 1, [RB, H[4]], [BK])
                for k in range(KC):
                    mm(bk[0:16, 256:384], hT[:, k, S:NT], wq3[:, k, :], k == 0, k == KC - 1, [RA, H[4]], [BK])
                cp("dve", kv_s[0:16, g, :, c * 128:(c + 1) * 128], bk[0:16, 0:256].rearrange("p (kv f) -> p kv f", kv=2), [BK], [R_kvs])
                act(qtok[0:16, g, c, :], bk[0:16, 256:384], AF.Copy, [BK], [R_qtok], scale=0.125)
                yield

            def gen_attn(c, g, B):
                biasp, R_bp, qT, kT, vaug, R_q, R_k, VA = (B[x] for x in ("biasp", "R_bp", "qT", "kT", "vaug", "R_q", "R_k", "VA"))
                d = DIL[g]
                nper = S // d
                nbr = nper // 128
                for hh in range(2):
                    pb = hh * 64
                    PTloc = {}
                    groups = []
                    curg, used = [], 0
                    for kb in range(16):
                        ncols = 256 if (kb % nbr) < nbr - 1 else 128
                        if used + ncols > 512:
                            groups.append(curg)
                            curg, used = [], 0
                        curg.append((kb, used, ncols))
                        used += ncols
                    groups.append(curg)
                    ob, OB = [None], [None]

                    def do_pv(qb):
                        slot = qb % 4
                        if slot == 0:
                            ob[0], OB[0] = nb()
                        b_in = qb % nbr
                        pt, RP, off = PTloc[qb]
                        mm(ob[0][0:65, slot * 128:(slot + 1) * 128], vaug[:, qb, hh, 0:65], pt[:, off:off + 128], True, b_in == 0,
                           [VA[qb], RP], [OB[0]])
                        if b_in > 0:
                            pt2, RP2, off2 = PTloc[qb - 1]
                            mm(ob[0][0:65, slot * 128:(slot + 1) * 128], vaug[:, qb - 1, hh, 0:65], pt2[:, off2 + 128:off2 + 256], False, True,
                               [VA[qb - 1], RP2], [OB[0]])
                        if slot == 3:
                            pi0 = (qb - 3) * 128
                            if d == 1:
                                dst = oacc[0:65, hh, pi0:pi0 + 512]
                                src = ob[0][0:65, :]
                            elif d == 4:
                                dst = oacc[0:65, hh, 0:S].rearrange("p (n r) -> p r n", r=d)[:, pi0 // nper, :]
                                src = ob[0][0:65, :]
                            else:
                                r0 = pi0 // nper
                                dst = oacc[0:65, hh, 0:S].rearrange("p (n r) -> p r n", r=d)[:, r0:r0 + 4, :]
                                src = ob[0][0:65, :].rearrange("p (a b) -> p a b", a=4)
                            if g == 0:
                                cp("dve", dst, src, [OB[0]], [OA[hh]])
                            else:
                                tt_op("dve", dst, dst, src, ALU.add, [OB[0], OA[hh]], [OA[hh]])

                    def emit_S(grp):
                        sb_, SB_ = nb()
                        pi_ = pti[0] % 4
                        pti[0] += 1
                        for kb, off, ncols in grp:
                            r_ = kb // nbr
                            n0_ = (kb % nbr) * 128
                            lo_ = r_ + n0_ * d
                            mm(sb_[:, off:off + ncols], kT[pb:pb + 64, lo_:lo_ + 127 * d + 1:d], qT[pb:pb + 64, lo_:lo_ + (ncols - 1) * d + 1:d],
                               True, True, [R_k, R_q], [SB_])
                        tot = grp[-1][1] + grp[-1][2]
                        nc0 = grp[0][2]
                        if len(grp) > 1 and all(x[2] == nc0 for x in grp):
                            nk = len(grp)
                            tt_op("dve", ssb[pi_ % 2][:, 0:tot].rearrange("p (a b) -> p a b", a=nk),
                                  sb_[:, 0:tot].rearrange("p (a b) -> p a b", a=nk),
                                  biasp[:, hh * 256:hh * 256 + nc0].unsqueeze(1).broadcast_to([128, nk, nc0]), ALU.add,
                                  [SB_, R_bp], [SSB[pi_ % 2]])
                        else:
                            for kb, off, ncols in grp:
                                tt_op("dve", ssb[pi_ % 2][:, off:off + ncols], sb_[:, off:off + ncols], biasp[:, hh * 256:hh * 256 + ncols], ALU.add,
                                      [SB_, R_bp], [SSB[pi_ % 2]])
                        act(ptb[pi_][:, 0:tot], ssb[pi_ % 2][:, 0:tot], AF.Exp, [SSB[pi_ % 2]], [PTB[pi_]])
                        for kb, off, ncols in grp:
                            PTloc[kb] = (ptb[pi_], PTB[pi_], off)

                    done_pv = 0
                    prev_last = -1
                    for grp in groups:
                        emit_S(grp)
                        while done_pv <= prev_last:
                            do_pv(done_pv)
                            done_pv += 1
                        prev_last = grp[-1][0]
                        yield
                    while done_pv <= prev_last:
                        do_pv(done_pv)
                        done_pv += 1
                    yield

            def finalize(c):
                for hh in range(2):
                    for t in range(4):
                        t0 = t * 512
                        act(rl[64:65, t0:t0 + 512], oacc[64:65, hh, t0:t0 + 512], AF.Ln, [OA[hh]], [RL4[t]])
                        act(rl[64:65, t0:t0 + 512], rl[64:65, t0:t0 + 512], AF.Exp, [RL4[t]], [RL4[t]], scale=-1.0)
                    for t in range(4):
                        t0 = t * 512
                        bk, BK = nb()
                        mm(bk[0:64, :], ones_f[64:65, 0:64], rl[64:65, t0:t0 + 512], True, True, [R_ones, RL4[t]], [BK])
                        tt_op("dve", attT[hh * 64:(hh + 1) * 64, c, t0:t0 + 512], oacc[0:64, hh, t0:t0 + 512], bk[0:64, :], ALU.mult,
                              [OA[hh], BK], [AT[c][t_] for t_ in tov(t0, t0 + 512)])

            stages = [(c, g) for c in range(2) for g in range(3)]
            for _ in gen_proj(stages[0][0], stages[0][1], BS[0], pre0):
                pass
            PRATIO = int(os.environ.get("MK_PRATIO", "2"))
            for i_, (c, g) in enumerate(stages):
                A_ = gen_attn(c, g, BS[i_ % 2])
                P_ = gen_proj(stages[i_ + 1][0], stages[i_ + 1][1], BS[(i_ + 1) % 2]) if i_ + 1 < len(stages) else iter(())
                a_done = p_done = False
                while not (a_done and p_done):
                    if not a_done:
                        try:
                            next(A_)
                        except StopIteration:
                            a_done = True
                    for _ in range(PRATIO):
                        if not p_done:
                            try:
                                next(P_)
                            except StopIteration:
                                p_done = True
                if g == 2:
                    finalize(c)

            if sub < 4:
                return
            cur[0] = o_bp
            o_sel = take(1024)
            sel = bv(o_sel, 2048)
            R_sel = Region("sel")
            o_self = take(2048)
            self32 = fv(o_self, 2048)
            R_self = Region("self32")
            o_kc = take(4 * 2048)
            kcs = [fv(o_kc + 2048 * i, 2048).rearrange("p (j f) -> p j f", j=4) for i in range(4)]
            KCS = [Region("kc%d" % i) for i in range(4)]
            o_pr = take(1024)
            prod = fv(o_pr, 1024)
            o_s16 = take(16)
            s16 = fv(o_s16, 16)
            o_ptc = take(16)
            ptc = fv(o_ptc, 16)
            o_ptn = take(64)
            ptn = fv(o_ptn, 64)
            o_rls = take(64)
            rls = fv(o_rls, 64)
            R_prod, R_s16, R_ptc, R_ptn, R_rls = (Region(n) for n in ("prod", "s16", "ptc", "ptn", "rls"))
            smp_regs = KCS + [R_prod, R_s16, R_ptc, R_ptn, R_rls, R_sel, R_self]
            alias(smp_regs, phase_regs[len(persist):])
            del phase_regs[len(persist):]
            phase_regs.extend(smp_regs)
            dma("sp", self32[0:16, :], consts_d[0:16, C_SEL:C_SEL + 2048], R_self, writes=[R_self])
            cp("act", sel[0:16, :], self32[0:16, :], [R_self], [R_sel])
            biasc = cst[:, C_BIASC:C_BIASC + 48]
            biasn = cst[0:16, C_BIASN:C_BIASN + 192]
            ss_ = float(os.environ.get("MK_SS", "99"))
            ba, BA = nb(reserve=True)
            P.op("dve", lambda e: e.memset(ba[0:64, 0:128], 0.0), writes=[BA])
            kci = 0
            o_pr2 = take(1024)
            prods = [prod, fv(o_pr2, 1024)]
            o_pc2 = take(16)
            ptcs = [ptc, fv(o_pc2, 16)]
            o_s2 = take(16)
            s16s = [s16, fv(o_s2, 16)]
            R_prods = [R_prod, Region("prod2")]
            R_ptcs = [R_ptc, Region("ptc2")]
            R_s16s = [R_s16, Region("s16b")]
            extra = [R_prods[1], R_ptcs[1], R_s16s[1]]
            alias(extra, att_regs)
            phase_regs.extend(extra)

            def front(i, g, kci):
                d = DIL[g]
                J = 1 if g == 0 else 4
                kc, RKC = kcs[kci % 4], KCS[kci % 4]
                prod_, R_prod_ = prods[kci % 2], R_prods[kci % 2]
                s16_, R_s16_ = s16s[kci % 2], R_s16s[kci % 2]
                ptc_, R_ptc_ = ptcs[kci % 2], R_ptcs[kci % 2]
                if g == 0:
                    dma("sp", kc[:, 0, :], caches[0][l, i], RKC, writes=[RKC])
                else:
                    dma("sp", kc, caches[g][l, i].rearrange("(n r) f -> n r f", r=d)[:, 0:4, :], RKC, writes=[RKC])
                qb4 = [nb() for _ in range(4)]
                for j in range(4):
                    qb_, QB_ = qb4[j]
                    tok = 4 * i + j
                    mm(qb_[:, 0:256], sel[0:16, tok * 128:(tok + 1) * 128],
                       qtok[0:16, g, :, :].rearrange("p c f -> p (c f)"), True, True, [R_sel, R_qtok], [QB_])
                for j in range(4):
                    qb_, QB_ = qb4[j]
                    tt_op("dve", prod_[:, j * 256:(j + 1) * 256], kc[:, j if J == 4 else 0, 0:256], qb_[:, 0:256],
                          ALU.mult, [RKC, QB_], [R_prod_])
                P.op("dve", lambda e: e.tensor_reduce(out=s16_, in_=prod_.rearrange("p (a e) -> p a e", e=64), axis=AX.X, op=ALU.add),
                     [R_prod_], [R_s16_])
                tt_op("dve", s16_, s16_, biasc[:, g * 16:(g + 1) * 16], ALU.add, [R_s16_, R_cst], [R_s16_])
                act(ptc_, s16_, AF.Exp, [R_s16_], [R_ptc_])

            def back(i, g, kci):
                J = 1 if g == 0 else 4
                kc, RKC = kcs[kci % 4], KCS[kci % 4]
                ptc_, R_ptc_ = ptcs[kci % 2], R_ptcs[kci % 2]
                mm(ba[0:64, 64 + i * 16:64 + (i + 1) * 16], ones_f[:, 0:64], ptc_, False, False, [R_ones, R_ptc_], [BA])
                if g == 0:
                    for h in range(4):
                        mm(ba[0:64, i * 16 + h:i * 16 + 16:4], kc[:, 0, 256 + h * 64:256 + (h + 1) * 64], ptc_[:, h:16:4], False, False,
                           [RKC, R_ptc_], [BA])
                else:
                    for j in range(4):
                        for h in range(4):
                            col = i * 16 + j * 4 + h
                            mm(ba[0:64, col:col + 1], kc[:, j, 256 + h * 64:256 + (h + 1) * 64], ptc_[:, j * 4 + h:j * 4 + h + 1], False, False,
                               [RKC, R_ptc_], [BA])

            steps = [(i, g) for i in range(4) for g in range(3)]
            for n_, (i, g) in enumerate(steps):
                front(i, g, n_)
                if n_ > 0:
                    back(steps[n_ - 1][0], steps[n_ - 1][1], n_ - 1)
            back(steps[-1][0], steps[-1][1], len(steps) - 1)
            for g in range(3 if ss_ >= 3 else 0):
                bn, BN = nb()
                for h in range(4):
                    ch = g * 2 + h // 2
                    mm(bn[0:16, h * 16:(h + 1) * 16], kTs[:, ch, :], qTs[:, ch, h % 2, :], True, True, [R_kTs, R_qTs], [BN])
                tt_op("dve", ptn[0:16, :], bn[0:16, 0:64], biasn[:, g * 64:(g + 1) * 64], ALU.add, [BN, R_cst], [R_ptn])
                act(ptn[0:16, :], ptn[0:16, :], AF.Exp, [R_ptn], [R_ptn])
                for h in range(4):
                    mm(ba[0:64, 64 + h:128:4], ones_f[0:16, 0:64], ptn[0:16, h * 16:(h + 1) * 16], False, False, [R_ones, R_ptn], [BA])
                    mm(ba[0:64, h:64:4], kv_s[0:16, g, 1, h * 64:(h + 1) * 64], ptn[0:16, h * 16:(h + 1) * 16], False, g == 2 and h == 3,
                       [R_kvs, R_ptn], [BA])
            reserved.clear()
            if ss_ < 4:
                return
            P.op("dve", lambda e: e.reciprocal(out=rls[0:64, :], in_=ba[0:64, 64:128]), [BA], [R_rls])
            for h in range(4):
                tt_op("dve", attT[(h % 2) * 64:(h % 2) * 64 + 64, h // 2, S:NT], ba[0:64, h:64:4], rls[0:64, h:64:4], ALU.mult,
                      [BA, R_rls], [AT[h // 2][4]])
            for g in range(3):
                dma("sp", kvs[g, l], kv_s[0:16, g, :, :].rearrange("p kv f -> p (kv f)"), R_kvs, reads=[R_kvs])

            if sub < 5:
                return
            cur[0] = mixer_base
            o_sp = take(4 * NT // 2)
            spT = bv(o_sp, 4 * NT).rearrange("p (c n) -> p c n", c=4)
            SPT = [[Region("sp%d_%d" % (c, t)) for t in range(5)] for c in range(4)]
            o_vn = take(512)
            vnorm_bc = fv(o_vn, 512)
            o_gb = take(512)
            gbias_bc = fv(o_gb, 512)
            R_vn, R_gb = Region("vn"), Region("gb")
            o_vt = take(17 * 512 // 2)
            vst = bv(o_vt, 17 * 512).rearrange("p (b f) -> p b f", b=17)
            VS = [Region("vs%d" % b) for b in range(17)]
            o_u = take(2 * NT // 2)
            uT = [bv(o_u + (NT // 2) * i, NT) for i in range(2)]
            UT = [Region("u%d" % i) for i in range(2)]
            o_gl = take(2 * 512)
            gl = [fv(o_gl + 512 * i, 512) for i in range(2)]
            GL = [Region("gl%d" % i) for i in range(2)]
            o_junk = take(512)
            junk = fv(o_junk, 512)
            R_junk = Region("junk")
            spt_flat = [r for row in SPT for r in row]
            g_regs = spt_flat + VS + UT + GL + [R_junk, R_vn, R_gb]
            alias(g_regs, phase_regs[len(persist):])
            del phase_regs[len(persist):]
            phase_regs.extend(g_regs)
            dma("sp", vnorm_bc, vnorm_d[l:l + 1, :].broadcast_to([128, 512]), R_vn, writes=[R_vn])
            dma("sp", gbias_bc, gb_d[l:l + 1, :].broadcast_to([128, 512]), R_gb, writes=[R_gb])
            o_wsl = o_gl
            wsl = fv(o_gl, 512).rearrange("p (g s) -> p g s", g=4)
            dma("sp", wsl, ws_d[l].rearrange("g t s -> t g s"), GL[0], writes=[GL[0]])
            bk, BK = nb()
            for gq in range(4):
                P.op("pe", lambda e, gq=gq, bk=bk: e.transpose(bk[:, gq * 128:(gq + 1) * 128], wsl[:, gq, :], ident), [GL[0], R_cst], [BK])
            tt_op("dve", wsT, bk[:, :].rearrange("p (g t) -> p g t", g=4), triu.unsqueeze(1).broadcast_to([128, 4, 128]), ALU.mult,
                  [BK, R_cst], [R_wsT])
            P.op("dve", lambda e: e.memset(wsSf, 0.0), writes=[R_wsSf])
            for i in range(4):
                for gq in range(4):
                    dma("sp", wsSf[4 * i:4 * i + 4, gq * 16 + 4 * i:gq * 16 + 4 * i + 4],
                        ws_d[l][gq, 0:4, 0:4].rearrange("t s -> s t"), R_wsSf, writes=[R_wsSf], slow=True)
            tt_op("dve", wsS[0:16, :, :], wsSf[0:16, :].rearrange("p (g t) -> p g t", g=4),
                  cst[0:16, C_MASKS:C_MASKS + 64].rearrange("p (g t) -> p g t", g=4), ALU.mult, [R_wsSf, R_cst], [R_wsS])
            v2 = lambda b: b.rearrange("p (k n) -> p k n", k=KC)
            wv = []
            for s_ in range(2):
                w_, RW_ = wload(v2, w_in[l][:, 2816 + s_ * 256:2816 + (s_ + 1) * 256].rearrange("(k p) n -> p k n", p=128))
                wv.append((v2(w_), RW_))
            ssq = small[:, 0:17]
            P.op("dve", lambda e: e.memset(ssq, 1.0), writes=[R_small])
            for blk in range(17):
                np_ = 128 if blk < 16 else 16
                lo = blk * 128
                hreg = None
                bk, BK = nb()
                for s_ in range(2):
                    for k in range(KC):
                        mm(bk[0:np_, s_ * 256:(s_ + 1) * 256], hT[:, k, lo:lo + np_], wv[s_][0][:, k, :], k == 0, k == KC - 1, [wv[s_][1]] + [H[t_] for t_ in tov(lo, lo + np_)], [BK])
                gi_ = blk % 2
                if blk < 16:
                    act(gl[gi_][0:np_, :], bk[0:np_, :], AF.Gelu_apprx_tanh, [BK], [GL[gi_]])
                    act(junk[0:np_, :], gl[gi_][0:np_, :], AF.Square, [GL[gi_]], [R_junk, R_small], accum=small[0:np_, blk:blk + 1])
                    tt_op("dve", vst[:, blk, :], gl[gi_], vnorm_bc, ALU.mult, [GL[gi_], R_vn], [VS[blk]])
                else:
                    act(vs_s[0:16, :], bk[0:16, :], AF.Gelu_apprx_tanh, [BK], [R_vss])
                    act(junk[0:16, :], vs_s[0:16, :], AF.Square, [R_vss], [R_junk, R_small], accum=small[0:16, 16:17])
            act(ssq, ssq, AF.Ln, [R_small], [R_small], scale=1.0 / 512, bias=EPS)
            act(ssq, ssq, AF.Exp, [R_small], [R_small], scale=-0.5)
            for blk in range(16):
                ts_op("dve", vst[:, blk, :], vst[:, blk, :], small[:, blk:blk + 1], ALU.mult, [VS[blk], R_small], [VS[blk]])
            stt(vs_s[0:16, :], vs_s[0:16, :], small[0:16, 16:17], vnorm_bc[0:16, :], ALU.mult, ALU.mult, [R_vss, R_small, R_vn], [R_vss])
            cp("act", vst[0:16, 16, :], vs_s[0:16, :], [R_vss], [VS[16]])
            dma("sp", gvs[l], vs_s[0:16, :], R_vss, reads=[R_vss])
            v1 = lambda b: b[:, 0:1024].rearrange("p (k n) -> p k n", k=KC)
            for gq in range(4):
                wu_, RWU = wload(v1, w_in[l][:, 2304 + gq * 128:2304 + (gq + 1) * 128].rearrange("(k p) n -> p k n", p=128))
                wu3 = v1(wu_)
                ui = gq % 2
                for t in range(5):
                    t0, n = TT[t]
                    bk, BK = nb()
                    for k in range(KC):
                        mm(bk[:, 0:n], wu3[:, k, :], hT[:, k, t0:t0 + n], k == 0, k == KC - 1, [RWU, H[t]], [BK])
                    act(uT[ui][:, t0:t0 + n], bk[:, 0:n], AF.Gelu_apprx_tanh, [BK], [UT[ui]])
                for t in range(5):
                    t0, n = LT[t]
                    bk, BK = nb()
                    if t < 4:
                        for j in range(4):
                            blk = 4 * t + j
                            mm(bk[:, j * 128:(j + 1) * 128], vst[:, blk, gq * 128:(gq + 1) * 128], wsT[:, gq, :], True, False, [VS[blk], R_wsT], [BK])
                            mm(bk[:, j * 128:(j + 1) * 128], ones_f[0:1, 0:128], gbias_bc[0:1, gq * 128:(gq + 1) * 128], False, True, [R_ones, R_gb], [BK])
                    else:
                        mm(bk[:, 0:16], vst[0:16, 16, gq * 128:(gq + 1) * 128], wsS[0:16, gq, :], True, False, [VS[16], R_wsS], [BK])
                        for i in range(4):
                            mm(bk[:, 4 * i:4 * i + 4], ones_f[0:1, 0:128], gbias_bc[0:1, gq * 128:gq * 128 + 4], False, i == 3, [R_ones, R_gb], [BK])
                    tt_op("dve", spT[:, gq, t0:t0 + n], uT[ui][:, t0:t0 + n], bk[:, 0:n], ALU.mult, [UT[ui], BK], [SPT[gq][t_] for t_ in tov(t0, t0 + n)])

            if sub < 6:
                return
            cur[0] = o_vn
            o_mx = take(KC * NT // 2)
            mxT = bv(o_mx, KC * NT).rearrange("p (k n) -> p k n", k=KC)
            MX = [[Region("mx%d_%d" % (k, t)) for t in range(5)] for k in range(KC)]
            o_ta = take(4 * 512)
            tmpA = [fv(o_ta + 512 * i, 512) for i in range(4)]
            TA = [Region("ta%d" % i) for i in range(4)]
            d_regs = [r for row in MX for r in row] + TA
            alias(d_regs, VS + UT + GL + [R_junk, R_vn, R_gb])
            del phase_regs[len(persist):]
            phase_regs.extend(spt_flat + d_regs)
            tai = 0
            for m2 in range(4):
                v2 = lambda b: b.rearrange("p (k n) -> p k n", k=KC)
                wga, RGA = wload(v2, w_in[l][:, 3328 + m2 * 256:3328 + (m2 + 1) * 256].rearrange("(k p) n -> p k n", p=128))
                wgb, RGB = wload(v2, w_in[l][:, 4352 + m2 * 256:4352 + (m2 + 1) * 256].rearrange("(k p) n -> p k n", p=128))
                vpa = lambda b: b[:, 0:512].rearrange("p (c n) -> p c n", c=2)
                vps = lambda b: b[:, 512:1536].rearrange("p (c n) -> p c n", c=4)
                wpa, RPA = wload(vpa, patt_d[l][:, m2 * 256:(m2 + 1) * 256].rearrange("(c p) n -> p c n", p=128))
                wps, RPS = wpa, RPA
                dma("pool", vps(wps), psp_d[l][:, m2 * 256:(m2 + 1) * 256].rearrange("(c p) n -> p c n", p=128), RPS, writes=[RPS])
                wga3, wgb3, wpa3, wps3 = v2(wga), v2(wgb), vpa(wpa), vps(wps)
                for mm_ in range(2):
                    m = m2 * 2 + mm_
                    ms = slice(mm_ * 128, (mm_ + 1) * 128)
                    for t in range(5):
                        t0, n = TT[t]
                        b1, B1 = nb()
                        for k in range(KC):
                            mm(b1[:, 0:n], wga3[:, k, ms], hT[:, k, t0:t0 + n], k == 0, k == KC - 1, [RGA, H[t]], [B1])
                        b2, B2 = nb()
                        for c in range(2):
                            mm(b2[:, 0:n], wpa3[:, c, ms], attT[:, c, t0:t0 + n], c == 0, c == 1, [RPA, AT[c][t]], [B2])
                        b3, B3 = nb()
                        for k in range(KC):
                            mm(b3[:, 0:n], wgb3[:, k, ms], hT[:, k, t0:t0 + n], k == 0, k == KC - 1, [RGB, H[t]], [B3])
                        b4, B4 = nb()
                        for c in range(4):
                            mm(b4[:, 0:n], wps3[:, c, ms], spT[:, c, t0:t0 + n], c == 0, c == 3, [RPS, SPT[c][t]], [B4])
                        ia, ib = tai % 4, (tai + 1) % 4
                        tai += 2
                        act(tmpA[ia][:, 0:n], b1[:, 0:n], AF.Sigmoid, [B1], [TA[ia]])
                        act(tmpA[ib][:, 0:n], b3[:, 0:n], AF.Sigmoid, [B3], [TA[ib]])
                        tt_op("dve", tmpA[ia][:, 0:n], tmpA[ia][:, 0:n], b2[:, 0:n], ALU.mult, [TA[ia], B2], [TA[ia]])
                        tt_op("dve", tmpA[ib][:, 0:n], tmpA[ib][:, 0:n], b4[:, 0:n], ALU.mult, [TA[ib], B4], [TA[ib]])
                        tt_op("dve", mxT[:, m, t0:t0 + n], tmpA[ia][:, 0:n], tmpA[ib][:, 0:n], ALU.add, [TA[ia], TA[ib]], [MX[m][t]])
            for m2 in range(4):
                v2 = lambda b: b.rearrange("p (k n) -> p k n", k=KC)
                wo, RO = wload(v2, wout_d[l][:, m2 * 256:(m2 + 1) * 256].rearrange("(k p) n -> p k n", p=128))
                wo3 = v2(wo)
                for mm_ in range(2):
                    m = m2 * 2 + mm_
                    for t in range(5):
                        t0, n = TT[t]
                        bk, BK = nb()
                        for k in range(KC):
                            mm(bk[:, 0:n], wo3[:, k, mm_ * 128:(mm_ + 1) * 128], mxT[:, k, t0:t0 + n], k == 0, k == KC - 1, [RO, MX[k][t]], [BK])
                        tt_op("dve", xT[:, m, t0:t0 + n], xT[:, m, t0:t0 + n], bk[:, 0:n], ALU.add, [BK, X[m][t]], [X[m][t]])

        def final():
            cur[0] = arena_base
            o_xt = take(2 * 1024)
            xtok = [fv(o_xt + 1024 * i, 1024) for i in range(2)]
            XTK = [Region("xtok%d" % i) for i in range(2)]
            o_yt = take(2 * 1024)
            ytok = [fv(o_yt + 1024 * i, 1024) for i in range(2)]
            YTK = [Region("ytok%d" % i) for i in range(2)]
            o_fg = take(1024)
            fg_bc = fv(o_fg, 1024)
            R_fg = Region("fg")
            o_jk = take(1024)
            junk = fv(o_jk, 1024)
            R_junk = Region("junkf")
            regs = XTK + YTK + [R_fg, R_junk]
            new_phase(regs)
            OUTR.extend(YTK)
            dma("sp", fg_bc, fnorm_d.broadcast_to([128, 1024]), R_fg, writes=[R_fg])
            for blk in range(17):
                np_ = 128 if blk < 16 else 16
                lo = blk * 128
                i = blk % 2
                b0, B0 = nb()
                b1, B1 = nb()
                for k in range(KC):
                    bk, BK = (b0, B0) if k < 4 else (b1, B1)
                    P.op("pe", lambda e, bk=bk, k=k, lo=lo, np_=np_: e.transpose(bk[0:np_, (k % 4) * 128:(k % 4 + 1) * 128], xT[:, k, lo:lo + np_], ident),
                         [X[k][t_] for t_ in tov(lo, lo + np_)] + [R_cst], [BK])
                cp("act", xtok[i][0:np_, 0:512], b0[0:np_, :], [B0], [XTK[i]])
                cp("dve", xtok[i][0:np_, 512:1024], b1[0:np_, :], [B1], [XTK[i]])
                ssc = small[0:np_, 32 + (blk % 8) * 2:32 + (blk % 8) * 2 + 1]
                act(junk[0:np_, :], xtok[i][0:np_, :], AF.Square, [XTK[i]], [R_junk, R_small], accum=ssc)
                ts_op("dve", ssc, ssc, 1.0 / D, ALU.mult, [R_small], [R_small], s2=EPS, op1=ALU.add)
                act(ssc, ssc, AF.Sqrt, [R_small], [R_small])
                P.op("dve", lambda e, ssc=ssc: e.reciprocal(out=ssc, in_=ssc), [R_small], [R_small])
                stt(ytok[i][0:np_, :], xtok[i][0:np_, :], ssc, fg_bc[0:np_, :], ALU.mult, ALU.mult, [XTK[i], R_small, R_fg], [YTK[i]])
                if blk < 16:
                    dma("sp", yp[lo:lo + 128, :], ytok[i], YTK[i], reads=[YTK[i]])
                else:
                    dma("sp", ys, ytok[i][0:16, :], YTK[i], reads=[YTK[i]])
            return regs

        fnorm_d = din("final_norm", [1, D])

        stage = int(os.environ.get("MK_STAGE", "99"))
        nstep = 0
        for l in range(DEPTH):
            for fn_, a_ in ((ffn, (l, 0)), (mixer, (l,)), (ffn, (l, 1))):
                if nstep < stage:
                    fn_(*a_)
                nstep += 1
        final()
        P.finish("sp", list(phase_regs) + OUTR)
        stats = P.emit(st)
        print("planner stats", stats)
    return nc


_CACHE = {}


def kernel(**inputs):
    f32 = lambda a: np.ascontiguousarray(np.asarray(a, dtype=np.float32))
    if "nc" not in _CACHE:
        _CACHE["nc"] = build_program()
        _CACHE["consts"] = build_consts()
    nc = _CACHE["nc"]
    consts = _CACHE["consts"]
    gl = []
    for l in range(DEPTH):
        for nm in ("ffn1_norm", "mix_norm", "ffn2_norm"):
            gl.append(f32(inputs[nm])[l])
    gl.append(f32(inputs["final_norm"]))
    gains = np.zeros((128, 56), np.float32)
    for i, v in enumerate(gl):
        gains[:, i * 8:(i + 1) * 8] = v.reshape(8, 128).T
    shared = {
        "gains": gains, "consts": consts,
        "gmlp_v_norm": f32(inputs["gmlp_v_norm"]),
        "gmlp_ws": f32(inputs["gmlp_ws"]),
        "gmlp_bias": f32(inputs["gmlp_bias"]).reshape(DEPTH, 512),
        "proj_att": f32(inputs["proj_att"]), "proj_spatial": f32(inputs["proj_spatial"]),
        "w_out": f32(inputs["w_out"]), "w_in": f32(inputs["w_in"]),
        "final_norm": f32(inputs["final_norm"]).reshape(1, D),
    }
    for nm in ("ffn1_gate", "ffn1_up", "ffn1_down", "ffn2_gate", "ffn2_up", "ffn2_down"):
        shared[nm] = f32(inputs[nm])
    xp = f32(inputs["x_prompt"])
    xs = f32(inputs["x_sample"])
    cks = [f32(inputs["cache_kv_w128"]), f32(inputs["cache_kv_w512"]), f32(inputs["cache_kv_w2048"])]
    in_maps = []
    for c in range(NCORES):
        m = dict(shared)
        m["xp"] = xp[c]
        m["xs"] = xs[4 * c:4 * c + 4].reshape(NS, D)
        for w, ck in zip(WIN, cks):
            m["c%d" % w] = np.ascontiguousarray(ck[:, 4 * c:4 * c + 4].reshape(DEPTH, 4, w, 512))
        in_maps.append(m)
    res = run_bass_kernel_spmd(nc, in_maps, core_ids=list(range(NCORES)))
    R = res.results
    y_prompt = np.stack([R[c]["yp"] for c in range(NCORES)], 0)
    y_sample = np.concatenate([R[c]["ys"].reshape(4, 4, D) for c in range(NCORES)], 0)
    outs = [y_prompt, y_sample]
    for gi, w in enumerate(WIN):
        a = np.stack([R[c]["kvp%d" % w] for c in range(NCORES)], 1)
        outs.append(a.reshape(DEPTH, NCORES, w, 2, 4, 64))
    for gi in range(3):
        a = np.concatenate([R[c]["kvs"][gi].reshape(DEPTH, 4, 4, 512) for c in range(NCORES)], 1)
        outs.append(a.reshape(DEPTH, 32, 4, 2, 4, 64))
    gv = np.concatenate([R[c]["gvs"].reshape(DEPTH, 4, 4, 512) for c in range(NCORES)], 1)
    outs.append(gv)
    return tuple(np.ascontiguousarray(o, dtype=np.float32) for o in outs)
```

```python
import bisect
import os
from contextlib import ExitStack

import numpy as np
import concourse.bass as bass
import concourse.mybir as mybir
from concourse.bass_utils import run_bass_kernel_spmd

F32 = mybir.dt.float32
BF16 = mybir.dt.bfloat16
AF = mybir.ActivationFunctionType
ALU = mybir.AluOpType
AX = mybir.AxisListType

NCORES = 8
D = 1024
KC = 8
S = 2048
NS = 16
NT = S + NS
DFF = 2816
NJ = 22
INW = 5376
DEPTH = 2
WIN = (128, 512, 2048)
DIL = (1, 4, 16)
EPS = 1e-6
NEG = -30000.0
TT = [(0, 400), (400, 400), (800, 400), (1200, 400), (1600, 464)]
LT = [(0, 512), (512, 512), (1024, 512), (1536, 512), (2048, 16)]


def tov(lo, hi):
    return [t for t, (t0, n) in enumerate(TT) if t0 < hi and t0 + n > lo]

ENGINES = ("pe", "act", "dve", "pool", "sp")
SAME_ENGINE_SYNC = bool(int(os.environ.get("MK_SES", "1")))
CAP = int(os.environ.get("MK_CAP", "500"))


class Region:
    __slots__ = ("name", "last_w", "readers", "sem", "dma_total", "excl")

    def __init__(self, name, excl=False):
        self.name = name
        self.excl = excl
        self.last_w = None
        self.readers = []
        self.sem = None
        self.dma_total = 0


class Planner:
    def __init__(self, nc):
        self.nc = nc
        self.ops = {e: [] for e in ENGINES}
        self.dma_sems = []

    def _deps(self, eng, reads, writes, is_dma):
        deps = []
        for r in reads:
            if r.last_w is not None:
                deps.append(r.last_w)
        for w in writes:
            if w.last_w is not None:
                deps.append(w.last_w)
            deps.extend(w.readers)
        out = []
        seen = set()
        for d in deps:
            key = (d[0], id(d[1]) if d[0] == "dma" else d[1], d[2])
            if key in seen:
                continue
            seen.add(key)
            if d[0] == "eng":
                if d[1] == eng and not is_dma and (eng == "pe" or not SAME_ENGINE_SYNC):
                    continue
                self.ops[d[1]][d[2]]["signal"] = True
            out.append(d)
        mx = {}
        for d in out:
            if d[0] == "dma":
                mx[id(d[1])] = max(mx.get(id(d[1]), 0), d[2])
        out = [d for d in out if d[0] != "dma" or d[2] == mx[id(d[1])]]
        return out

    def op(self, eng, fn, reads=(), writes=()):
        ex = [r for r in reads if r.excl]
        if ex:
            writes = list(writes) + [r for r in ex if r not in writes]
            reads = [r for r in reads if not r.excl]
        deps = self._deps(eng, reads, writes, False)
        seq = len(self.ops[eng])
        self.ops[eng].append(dict(kind="op", fn=fn, deps=deps, signal=False))
        me = ("eng", eng, seq)
        for r in reads:
            r.readers = [x for x in r.readers if not (x[0] == "eng" and x[1] == eng)]
            r.readers.append(me)
        for w in writes:
            w.last_w = me
            w.readers = []

    def dma(self, eng, fn, owner, reads=(), writes=()):
        deps = self._deps(eng, reads, writes, True)
        if owner.sem is None or owner.dma_total >= 480:
            h = Region(owner.name)
            h.sem = "pending"
            self.dma_sems.append(h)
            owner.sem = h
            owner.dma_total = 0
        owner.dma_total += 16
        holder = owner.sem
        self.ops[eng].append(dict(kind="dma", fn=fn, deps=deps, signal=False, owner=holder))
        me = ("dma", holder, owner.dma_total)
        for r in reads:
            r.readers.append(me)
        for w in writes:
            w.last_w = me
            w.readers = []

    def finish(self, eng, regions):
        deps = self._deps(eng, regions, regions, True)
        self.ops[eng].append(dict(kind="nop", fn=None, deps=deps, signal=False))

    def emit(self, stack):
        nc = self.nc
        sig_seq = {}
        for e in ENGINES:
            sig_seq[e] = [i for i, r in enumerate(self.ops[e]) if r["signal"]]
        eng_sems = {}
        for e in ENGINES:
            n = (len(sig_seq[e]) + CAP - 1) // CAP
            eng_sems[e] = [stack.enter_context(nc.semaphore("s_%s_%d" % (e, i))) for i in range(n)]
        for i_, rg in enumerate(self.dma_sems):
            rg.sem = stack.enter_context(nc.semaphore("d%d_%s" % (i_, rg.name)))
        print("semaphores:", len(self.dma_sems), "dma +", sum(len(v) for v in eng_sems.values()), "engine")

        def sig_index(e, seq):
            i = bisect.bisect_left(sig_seq[e], seq)
            assert i < len(sig_seq[e]) and sig_seq[e][i] == seq, (e, seq)
            return i

        eng_obj = {"pe": "tensor", "act": "scalar", "dve": "vector", "pool": "gpsimd", "sp": "sync"}
        stats = {}

        def run_engine(e, engine):
            seen_eng = {x: -1 for x in ENGINES}
            seen_dma = {}
            nwait = 0
            my_sig = 0
            for rec in self.ops[e]:
                for d in rec["deps"]:
                    if d[0] == "eng":
                        k = sig_index(d[1], d[2])
                        if k <= seen_eng[d[1]]:
                            continue
                        seen_eng[d[1]] = k
                        engine.wait_ge(eng_sems[d[1]][k // CAP], k % CAP + 1)
                        nwait += 1
                    else:
                        rg, val = d[1], d[2]
                        if seen_dma.get(id(rg), 0) >= val:
                            continue
                        seen_dma[id(rg)] = val
                        engine.wait_ge(rg.sem, val)
                        nwait += 1
                if rec["kind"] == "nop":
                    continue
                ins = rec["fn"](engine)
                if rec["kind"] == "dma":
                    ins.then_inc(rec["owner"].sem, 16)
                if rec["signal"]:
                    k = my_sig
                    my_sig += 1
                    ins.then_inc(eng_sems[e][k // CAP], 1)
            stats[e] = (len(self.ops[e]), nwait, my_sig)

        with nc.Block() as block:
            for e in ENGINES:
                if self.ops[e]:
                    getattr(block, eng_obj[e])(lambda engine, e=e: run_engine(e, engine))
        return stats


def alibi_slopes():
    h = np.arange(1, 13, dtype=np.float32)
    return np.power(np.float32(2.0), -8.0 * h / 12).astype(np.float32).reshape(3, 4)


C_IDENT = 0
C_TRIU = 128
C_BIASC = 256
C_BIASN = C_BIASC + 48
C_MASKS = C_BIASN + 192
C_RES = C_MASKS + 64
C_SEL = C_RES
C_BIASP = C_SEL + 2048
C_TOT = C_BIASP + 12 * 256


def build_consts():
    c = np.zeros((128, C_TOT), np.float32)
    c[:, C_IDENT:C_IDENT + 128] = np.eye(128, dtype=np.float32)
    s_i = np.arange(128)[:, None]
    t_i = np.arange(128)[None, :]
    c[:, C_TRIU:C_TRIU + 128] = (s_i <= t_i).astype(np.float32)
    sl = alibi_slopes()
    kk = np.arange(128)[:, None].astype(np.float32)
    aa = np.arange(128)[None, :].astype(np.float32)
    for g in range(3):
        for h in range(4):
            m = sl[g, h] * DIL[g]
            diag = np.where(aa >= kk, -m * (aa - kk), NEG)
            prev = np.where(kk >= aa, -m * (128.0 + aa - kk), NEG)
            o = C_BIASP + (g * 4 + h) * 256
            c[:, o:o + 128] = diag
            c[:, o + 128:o + 256] = prev
    n = np.arange(128).astype(np.float32)
    for g in range(3):
        for j in range(4):
            for h in range(4):
                col = C_BIASC + g * 16 + j * 4 + h
                if g == 0:
                    c[:, col] = np.where(n >= j, -sl[0, h] * (128.0 + j - n), NEG)
                else:
                    c[:, col] = -sl[g, h] * DIL[g] * (128.0 - n)
    for g in range(3):
        for h in range(4):
            for q in range(16):
                for k in range(16):
                    col = C_BIASN + g * 64 + h * 16 + q
                    i, j = divmod(q, 4)
                    i2, j2 = divmod(k, 4)
                    if i != i2:
                        v = NEG
                    elif g == 0:
                        v = -sl[0, h] * (j - j2) if j2 <= j else NEG
                    else:
                        v = 0.0 if j2 == j else NEG
                    c[k, col] = v
    for g in range(4):
        for k in range(16):
            for t in range(16):
                i, s = divmod(k, 4)
                i2, t2 = divmod(t, 4)
                c[k, C_MASKS + g * 16 + t] = 1.0 if (i == i2 and s <= t2) else 0.0
    for j in range(16):
        c[j, C_SEL + j * 128:C_SEL + (j + 1) * 128] = 1.0
    return c


def build_program():
    nc = bass.Bass("TRN2", target_bir_lowering=False)

    def din(name, shape):
        return nc.dram_tensor(name, list(shape), F32, kind="ExternalInput").ap()

    def dout(name, shape):
        return nc.dram_tensor(name, list(shape), F32, kind="ExternalOutput").ap()

    xp = din("xp", [S, D])
    xs = din("xs", [NS, D])
    c128 = din("c128", [DEPTH, 4, 128, 512])
    c512 = din("c512", [DEPTH, 4, 512, 512])
    c2048 = din("c2048", [DEPTH, 4, 2048, 512])
    caches = (c128, c512, c2048)
    gains_d = din("gains", [128, 56])
    consts_d = din("consts", [128, C_TOT])
    w_gate = (din("ffn1_gate", [DEPTH, D, DFF]), din("ffn2_gate", [DEPTH, D, DFF]))
    w_up = (din("ffn1_up", [DEPTH, D, DFF]), din("ffn2_up", [DEPTH, D, DFF]))
    w_down = (din("ffn1_down", [DEPTH, DFF, D]), din("ffn2_down", [DEPTH, DFF, D]))
    w_in = din("w_in", [DEPTH, D, INW])
    vnorm_d = din("gmlp_v_norm", [DEPTH, 512])
    ws_d = din("gmlp_ws", [DEPTH, 4, 128, 128])
    gb_d = din("gmlp_bias", [DEPTH, 512])
    patt_d = din("proj_att", [DEPTH, 256, D])
    psp_d = din("proj_spatial", [DEPTH, 512, D])
    wout_d = din("w_out", [DEPTH, D, D])

    yp = dout("yp", [S, D])
    ys = dout("ys", [NS, D])
    kvp = (dout("kvp128", [DEPTH, 128, 512]), dout("kvp512", [DEPTH, 512, 512]),
           dout("kvp2048", [DEPTH, 2048, 512]))
    kvs = dout("kvs", [3, DEPTH, NS, 512])
    gvs = dout("gvs", [DEPTH, NS, 512])

    with ExitStack() as st:
        ARENA_W = 53100
        arena = st.enter_context(nc.sbuf_tensor("arena", [128, ARENA_W], F32))
        banks = [st.enter_context(nc.psum_tensor("bank%d" % i, [128, 512], F32)) for i in range(8)]
        BANK = [Region("bank%d" % i, excl=True) for i in range(8)]
        P = Planner(nc)

        cur = [0]

        def take(nwords):
            o = cur[0]
            cur[0] += (nwords + 1) // 2 * 2
            assert cur[0] <= ARENA_W, cur[0]
            return o

        def fv(off, n):
            return arena[:, off:off + n]

        def bv(off, n):
            v = arena[:, off:off + n // 2].bitcast(BF16)
            assert tuple(v.shape) == (128, n), v.shape
            return v

        o_x = take(KC * NT)
        xT = fv(o_x, KC * NT).rearrange("p (k n) -> p k n", k=KC)
        o_h = take(KC * NT // 2)
        hT = bv(o_h, KC * NT).rearrange("p (k n) -> p k n", k=KC)
        o_c = take(C_RES)
        cst = fv(o_c, C_RES)
        o_g = take(56)
        gains = fv(o_g, 56)
        o_mh = take(512)
        mhalf = fv(o_mh, 512)
        o_ob = take(64)
        ones_bf = bv(o_ob, 128)
        o_of = take(128)
        ones_f = fv(o_of, 128)
        o_ws = take(256)
        wsT = bv(o_ws, 512).rearrange("p (g t) -> p g t", g=4)
        o_wss = take(32)
        wsS = bv(o_wss, 64).rearrange("p (g t) -> p g t", g=4)
        o_wsf = take(64)
        wsSf = fv(o_wsf, 64)
        o_sm = take(64)
        small = fv(o_sm, 64)
        NWP = 5
        o_wp = [take(1024) for _ in range(NWP)]
        wp = [bv(o, 2048) for o in o_wp]
        WP = [Region("wp%d" % i) for i in range(NWP)]
        R_cst, R_gains, R_ones, R_mh = (Region(n) for n in ("cst", "gains", "ones", "mh"))
        R_wsT, R_wsS, R_wsSf, R_small = (Region(n) for n in ("wsT", "wsS", "wsSf", "small"))
        OUTR = []
        X = [[Region("x%d_%d" % (k, t)) for t in range(5)] for k in range(KC)]
        H = [Region("h%d" % t) for t in range(5)]
        arena_base = cur[0]

        ident = cst[:, C_IDENT:C_IDENT + 128]
        triu = cst[:, C_TRIU:C_TRIU + 128]

        bank_i = [0]

        reserved = set()

        def nb(reserve=False):
            i = bank_i[0]
            while i in reserved:
                i = (i + 1) % 8
            bank_i[0] = (i + 1) % 8
            if reserve:
                reserved.add(i)
            return banks[i], BANK[i]

        wp_i = [0]

        def wload(view_fn, src, eng="pool"):
            i = wp_i[0]
            wp_i[0] = (i + 1) % NWP
            dst = view_fn(wp[i])
            P.dma(eng, lambda e: e.dma_start(out=dst, in_=src), WP[i], writes=[WP[i]])
            return wp[i], WP[i]

        def mm(out, lhsT, rhs, start, stop, reads, writes):
            P.op("pe", lambda e: e.matmul(out, lhsT=lhsT, rhs=rhs, start=start, stop=stop,
                                          skip_group_check=True), reads, writes)

        def act(out, in_, func, reads, writes, scale=None, bias=None, accum=None):
            kw = {}
            if scale is not None:
                kw["scale"] = scale
            if bias is not None:
                kw["bias"] = bias
            if accum is not None:
                kw["accum_out"] = accum
            P.op("act", lambda e: e.activation(out=out, in_=in_, func=func, **kw), reads, writes)

        def tt_op(eng, out, in0, in1, op, reads, writes):
            P.op(eng, lambda e: e.tensor_tensor(out=out, in0=in0, in1=in1, op=op), reads, writes)

        def ts_op(eng, out, in0, s1, op0, reads, writes, s2=None, op1=None):
            if op1 is None:
                P.op(eng, lambda e: e.tensor_scalar(out=out, in0=in0, scalar1=s1, scalar2=None, op0=op0), reads, writes)
            else:
                P.op(eng, lambda e: e.tensor_scalar(out=out, in0=in0, scalar1=s1, scalar2=s2, op0=op0, op1=op1), reads, writes)

        def stt(out, in0, scalar, in1, op0, op1, reads, writes):
            P.op("dve", lambda e: e.scalar_tensor_tensor(out=out, in0=in0, scalar=scalar, in1=in1, op0=op0, op1=op1),
                 reads, writes)

        def cp(eng, out, in_, reads, writes):
            if eng == "act":
                P.op("act", lambda e: e.copy(out=out, in_=in_), reads, writes)
            else:
                P.op(eng, lambda e: e.tensor_copy(out=out, in_=in_), reads, writes)

        def dma(eng, out, in_, owner, reads=(), writes=(), slow=False):
            if slow:
                P.dma(eng, lambda e: e.dma_start(out=out, in_=in_, allow_slow_non_contiguous=True), owner, reads, writes)
            else:
                P.dma(eng, lambda e: e.dma_start(out=out, in_=in_), owner, reads, writes)

        def alias(new_regions, old_regions):
            acc = []
            for o in old_regions:
                if o.last_w is not None:
                    acc.append(o.last_w)
                acc.extend(o.readers)
            for n in new_regions:
                n.last_w = None
                n.readers = list(acc) + n.readers

        phase_regs = []

        def new_phase(regs):
            alias(regs, phase_regs)
            del phase_regs[:]
            phase_regs.extend(regs)

        dma("sp", cst, consts_d[:, 0:C_RES], R_cst, writes=[R_cst])
        P.op("dve", lambda e: e.memset(mhalf, -0.5), writes=[R_mh])
        dma("sp", gains, gains_d, R_gains, writes=[R_gains])
        P.op("dve", lambda e: e.memset(ones_bf, 1.0), writes=[R_ones])
        P.op("dve", lambda e: e.memset(ones_f, 1.0), writes=[R_ones])

        cur[0] = arena_base
        o_stg = take(2 * 4 * 1024)
        xstgs = [fv(o_stg + 4096 * i_, 4096).rearrange("p (b f) -> p b f", b=4) for i_ in range(2)]
        R_xstgs = [Region("xstg%d" % i_) for i_ in range(2)]
        new_phase(R_xstgs)
        for t in range(5):
            xstg, R_xstg = xstgs[t % 2], R_xstgs[t % 2]
            t0, n = LT[t]
            if t < 4:
                dma("sp", xstg, xp[t0:t0 + 512, :].rearrange("(b p) f -> p b f", p=128), R_xstg, writes=[R_xstg])
            else:
                dma("sp", xstg[0:16, 0, :], xs, R_xstg, writes=[R_xstg])
            for k in range(KC):
                bk, BK = nb()
                if t < 4:
                    for b in range(4):
                        P.op("pe", lambda e, bk=bk, b=b, k=k, xstg=xstg: e.transpose(bk[:, b * 128:(b + 1) * 128], xstg[:, b, k * 128:(k + 1) * 128], ident),
                             [R_xstg, R_cst], [BK])
                else:
                    P.op("pe", lambda e, bk=bk, k=k, xstg=xstg: e.transpose(bk[:, 0:16], xstg[0:16, 0, k * 128:(k + 1) * 128], ident[0:16, 0:16]),
                         [R_xstg, R_cst], [BK])
                cp("act" if k % 2 else "dve", xT[:, k, t0:t0 + n], bk[:, 0:n], [BK], [X[k][t_] for t_ in tov(t0, t0 + n)])

        def rmsnorm(gi):
            o0 = cur[0]
            o_sq = take(4 * 256)
            sq = [bv(o_sq + 256 * i, 512) for i in range(4)]
            R_sq = [Region("sq%d" % i) for i in range(4)]
            o_rs = take(512)
            rs = fv(o_rs, 512)
            R_rs = Region("rs")
            new_regs = R_sq + [R_rs]
            alias(new_regs, phase_regs)
            phase_regs.extend(new_regs)
            for t in range(5):
                t0, n = TT[t]
                bk, BK = nb()
                for k in range(KC):
                    i = k % 4
                    if k % 2 == 0:
                        act(sq[i][:, 0:n], xT[:, k, t0:t0 + n], AF.Square, [X[k][t]], [R_sq[i]])
                    else:
                        tt_op("pool", sq[i][:, 0:n], xT[:, k, t0:t0 + n], xT[:, k, t0:t0 + n], ALU.mult, [X[k][t]], [R_sq[i]])
                    mm(bk[:, 0:n], ones_bf, sq[i][:, 0:n], k == 0, k == KC - 1, [R_sq[i], R_ones], [BK])
                act(rs[:, 0:n], bk[:, 0:n], AF.Ln, [BK], [R_rs], scale=1.0 / D, bias=EPS)
                act(bk[:, 0:n], rs[:, 0:n], AF.Exp, [R_rs], [BK], scale=-0.5)
                for k in range(KC):
                    stt(hT[:, k, t0:t0 + n], xT[:, k, t0:t0 + n], gains[:, gi * 8 + k:gi * 8 + k + 1], bk[:, 0:n],
                        ALU.mult, ALU.mult, [X[k][t], R_gains, BK], [H[t]])
            cur[0] = o0

        def ffn(l, which):
            gi = l * 3 + (0 if which == 0 else 2)
            cur[0] = arena_base
            HC = 6
            o_hid = take(HC * NT // 2)
            hid = bv(o_hid, HC * NT).rearrange("p (j n) -> p j n", j=HC)
            HID = [[Region("hid%d_%d" % (j, t)) for t in range(5)] for j in range(HC)]
            o_sg = take(2 * 512)
            sg = [fv(o_sg + 512 * i, 512) for i in range(2)]
            R_sg = [Region("sg%d" % i) for i in range(2)]
            new_phase([r for row in HID for r in row] + R_sg)
            Wg, Wu, Wd = w_gate[which][l], w_up[which][l], w_down[which][l]
            vfn0 = lambda b: b.rearrange("p (k n) -> p k n", k=KC)
            pre_gu = {0: (wload(vfn0, Wg[:, 0:256].rearrange("(k p) n -> p k n", p=128)),
                          wload(vfn0, Wu[:, 0:256].rearrange("(k p) n -> p k n", p=128)))}
            rmsnorm(gi)
            sgi = 0
            for j0 in range(0, NJ, HC):
                nj = min(HC, NJ - j0)
                for s0 in range(0, nj, 2):
                    ja = j0 + s0
                    vfn = lambda b: b.rearrange("p (k n) -> p k n", k=KC)
                    if ja in pre_gu:
                        (wg, RG), (wu, RU) = pre_gu.pop(ja)
                    else:
                        wg, RG = wload(vfn, Wg[:, ja * 128:(ja + 2) * 128].rearrange("(k p) n -> p k n", p=128))
                        wu, RU = wload(vfn, Wu[:, ja * 128:(ja + 2) * 128].rearrange("(k p) n -> p k n", p=128))
                    wg3, wu3 = vfn(wg), vfn(wu)
                    for jj in range(2):
                        jl = s0 + jj
                        for t in range(5):
                            t0, n = TT[t]
                            bg, BG = nb()
                            for k in range(KC):
                                mm(bg[:, 0:n], wg3[:, k, jj * 128:(jj + 1) * 128], hT[:, k, t0:t0 + n], k == 0, k == KC - 1, [RG, H[t]], [BG])
                            bu, BU = nb()
                            for k in range(KC):
                                mm(bu[:, 0:n], wu3[:, k, jj * 128:(jj + 1) * 128], hT[:, k, t0:t0 + n], k == 0, k == KC - 1, [RU, H[t]], [BU])
                            i = sgi % 2
                            sgi += 1
                            act(sg[i][:, 0:n], bg[:, 0:n], AF.Silu, [BG], [R_sg[i]])
                            tt_op("dve", hid[:, jl, t0:t0 + n], sg[i][:, 0:n], bu[:, 0:n], ALU.mult, [R_sg[i], BU], [HID[jl][t]])
                wds = []
                for s0 in range(0, nj, 2):
                    ja = j0 + s0
                    vfn = lambda b: b.rearrange("p (j n) -> p j n", j=2)
                    wd, RD = wload(vfn, Wd[ja * 128:(ja + 2) * 128, :].rearrange("(j p) n -> p j n", p=128))
                    wds.append((vfn(wd), RD))
                last_part = (j0 + HC >= NJ)
                order = [(m, t) for t in range(5) for m in range(KC)] if last_part else [(m, t) for m in range(KC) for t in range(5)]
                for m, t in order:
                    if True:
                        t0, n = TT[t]
                        bk, BK = nb()
                        for jl in range(nj):
                            wd3, RD = wds[jl // 2]
                            mm(bk[:, 0:n], wd3[:, jl % 2, m * 128:(m + 1) * 128], hid[:, jl, t0:t0 + n], jl == 0, jl == nj - 1,
                               [RD, HID[jl][t]], [BK])
                        stt(xT[:, m, t0:t0 + n], bk[:, 0:n], 0.5, xT[:, m, t0:t0 + n], ALU.mult, ALU.add, [BK, X[m][t]], [X[m][t]])

        def mixer(l):
            cur[0] = arena_base
            o_att = take(2 * NT // 2)
            attT = bv(o_att, 2 * NT).rearrange("p (c n) -> p c n", c=2)
            AT = [[Region("att%d_%d" % (c, t)) for t in range(5)] for c in range(2)]
            o_qts = take(6 * 2 * 16 // 2)
            qTs = bv(o_qts, 192).rearrange("p (c h n) -> p c h n", c=6, h=2)
            o_kts = take(6 * 16 // 2)
            kTs = bv(o_kts, 96).rearrange("p (c n) -> p c n", c=6)
            o_qtok = take(768 // 2)
            qtok = bv(o_qtok, 768).rearrange("p (g c f) -> p g c f", g=3, c=2)
            o_kvs = take(3 * 512)
            kv_s = fv(o_kvs, 1536).rearrange("p (g kv f) -> p g kv f", g=3, kv=2)
            R_qTs, R_kTs, R_qtok, R_kvs = Region("qTs"), Region("kTs"), Region("qtok"), Region("kvs")
            o_vss = take(512)
            vs_s = fv(o_vss, 512)
            R_vss = Region("vs_s")
            mixer_base = cur[0]
            persist = [r for row in AT for r in row] + [R_qTs, R_kTs, R_qtok, R_kvs, R_vss]
            new_phase(persist)
            OUTR.extend([R_kvs, R_vss])
            P.op("dve", lambda e: e.memset(qTs, 0.0), writes=[R_qTs])

            def proj_loads(c, g):
                qc0 = g * 256 + c * 128
                v1 = lambda b: b[:, 0:1024].rearrange("p (k n) -> p k n", k=KC)
                v2 = lambda b: b.rearrange("p (k n) -> p k n", k=KC)
                wa, RA = wload(v1, w_in[l][:, qc0:qc0 + 128].rearrange("(k p) n -> p k n", p=128))
                wq3 = v1(wa)
                i = wp_i[0]
                wp_i[0] = (i + 1) % NWP
                wb3 = v2(wp[i])
                RB = WP[i]
                dma("pool", wb3[:, :, 0:128], w_in[l][:, 768 + qc0:768 + qc0 + 128].rearrange("(k p) n -> p k n", p=128), RB, writes=[RB])
                dma("pool", wb3[:, :, 128:256], w_in[l][:, 1536 + qc0:1536 + qc0 + 128].rearrange("(k p) n -> p k n", p=128), RB, writes=[RB])
                return wq3, RA, wb3, RB

            pre0 = proj_loads(0, 0)
            rmsnorm(l * 3 + 1)
            sub = int(os.environ.get("MK_SUB", "99"))
            if sub < 2:
                return

            o_bp = cur[0]
            BS = []
            for si_ in range(2):
                b_ = {}
                ob_ = take(512)
                b_["biasp"] = fv(ob_, 512)
                b_["R_bp"] = Region("bp%d" % si_)
                oq_ = take(NT // 2)
                b_["qT"] = bv(oq_, NT)
                ok_ = take(NT // 2)
                b_["kT"] = bv(ok_, NT)
                ov_ = take(16 * 2 * 66 // 2)
                b_["vaug"] = bv(ov_, 16 * 132).rearrange("p (b h e) -> p b h e", b=16, h=2)
                b_["R_q"], b_["R_k"] = Region("qT%d" % si_), Region("kT%d" % si_)
                b_["VA"] = [Region("va%d_%d" % (si_, b)) for b in range(16)]
                BS.append(b_)
            o_oa = take(2 * NT)
            oacc = fv(o_oa, 2 * NT).rearrange("p (h n) -> p h n", h=2)
            o_pt = take(4 * 256)
            ptb = [bv(o_pt + 256 * i, 512) for i in range(4)]
            o_st = take(2 * 256)
            stg = [fv(o_st + 256 * i, 256) for i in range(2)]
            o_rl = take(NT)
            rl = fv(o_rl, NT)
            o_ssb = take(2 * 512)
            ssb = [fv(o_ssb + 512 * i, 512) for i in range(2)]
            SSB = [Region("ssb%d" % i) for i in range(2)]
            R_vones = Region("vones")
            OA = [Region("oa%d" % h) for h in range(2)]
            PTB = [Region("pt%d" % i) for i in range(4)]
            STG = [Region("stg%d" % i) for i in range(2)]
            R_rl = Region("rl")
            RL4 = [Region("rl%d" % i) for i in range(4)]
            att_regs = [R_vones, R_rl] + RL4 + OA + PTB + STG + SSB
            for b_ in BS:
                att_regs += [b_["R_q"], b_["R_k"], b_["R_bp"]] + b_["VA"]
            OUTR.extend(STG)
            alias(att_regs, phase_regs[len(persist):])
            del phase_regs[len(persist):]
            phase_regs.extend(att_regs)
            for b_ in BS:
                P.op("dve", lambda e, b_=b_: e.memset(b_["vaug"][:, :, :, 64:65], 1.0), writes=[R_vones] + b_["VA"])
            pti = [0]
            sti = [0]

            def gen_proj(c, g, B, pre=None):
                biasp, R_bp, qT, kT, vaug, R_q, R_k, VA = (B[x] for x in ("biasp", "R_bp", "qT", "kT", "vaug", "R_q", "R_k", "VA"))
                d = DIL[g]
                nper = S // d
                nbr = nper // 128
                wq3, RA, wb3, RB = pre if pre is not None else proj_loads(c, g)
                dma("sp", biasp.rearrange("p (h f) -> p h f", h=2),
                    consts_d[:, C_BIASP + (g * 4 + 2 * c) * 256:C_BIASP + (g * 4 + 2 * c + 2) * 256].rearrange("p (h f) -> p h f", h=2),
                    R_bp, writes=[R_bp])
                for which, w3, RW, dstT, RT, dsts, RS_, scale in ((0, wq3, RA, qT, R_q, qTs, R_qTs, 0.125),
                                                                 (1, wb3, RB, kT, R_k, kTs, R_kTs, 1.0)):
                    for t in range(5):
                        t0, n = TT[t]
                        bk, BK = nb()
                        for k in range(KC):
                            mm(bk[:, 0:n], w3[:, k, 0:128], hT[:, k, t0:t0 + n], k == 0, k == KC - 1, [RW, H[t]], [BK])
                        npr = min(t0 + n, S) - t0
                        dst = dstT[:, t0:t0 + npr]
                        src = bk[:, 0:npr]
                        act(dst, src, AF.Copy, [BK], [RT], scale=scale)
                        if npr < n:
                            if which == 0:
                                act(dsts[0:64, g * 2 + c, 0, :], bk[0:64, npr:n], AF.Copy, [BK], [RS_], scale=scale)
                                act(dsts[64:128, g * 2 + c, 1, :], bk[64:128, npr:n], AF.Copy, [BK], [RS_], scale=scale)
                            else:
                                act(dsts[:, g * 2 + c, :], bk[:, npr:n], AF.Copy, [BK], [RS_], scale=scale)
                        yield
                for blk in range(16):
                    r = blk // nbr
                    n0 = (blk % nbr) * 128
                    lo = r + n0 * d
                    hi = r + (n0 + 127) * d + 1
                    need_out = (g == 2) or (g == 1 and blk % nbr == nbr - 1) or (g == 0 and blk == 15)
                    hregs = [H[t] for t in tov(lo, hi)]
                    bk, BK = nb()
                    c0 = 0 if need_out else 128
                    for k in range(KC):
                        mm(bk[:, c0:256], hT[:, k, lo:hi:d], wb3[:, k, c0:256], k == 0, k == KC - 1, [RB] + hregs, [BK])
                    cp("act", vaug[:, blk, :, 0:64], bk[:, 128:256].rearrange("p (h e) -> p h e", h=2), [BK], [VA[blk]])
                    if need_out:
                        si = sti[0] % 2
                        sti[0] += 1
                        cp("dve", stg[si], bk[:, 0:256], [BK], [STG[si]])
                        row0 = lo - (S - WIN[g])
                        dst = kvp[g][l].rearrange("t (kv f) -> t kv f", kv=2)[row0:row0 + 127 * d + 1:d, :, c * 128:(c + 1) * 128]
                        dma("sp", dst, stg[si].rearrange("p (kv f) -> p kv f", kv=2), STG[si], reads=[STG[si]])
                    yield
                bk, BK = nb()
                for k in range(KC):
                    mm(bk[0:16, 0:256], hT[:, k, S:NT], wb3[:, k, :], k == 0, k == KC - 1, [RB, H[4]], [BK])
                for k in range(KC):
                    mm(bk[0:16, 256:384], hT[:, k, S:NT], wq3[:, k, :], k == 0, k == KC - 1, [RA, H[4]], [BK])
                cp("dve", kv_s[0:16, g, :, c * 128:(c + 1) * 128], bk[0:16, 0:256].rearrange("p (kv f) -> p kv f", kv=2), [BK], [R_kvs])
                act(qtok[0:16, g, c, :], bk[0:16, 256:384], AF.Copy, [BK], [R_qtok], scale=0.125)
                yield

            def gen_attn(c, g, B):
                biasp, R_bp, qT, kT, vaug, R_q, R_k, VA = (B[x] for x in ("biasp", "R_bp", "qT", "kT", "vaug", "R_q", "R_k", "VA"))
                d = DIL[g]
                nper = S // d
                nbr = nper // 128
                for hh in range(2):
                    pb = hh * 64
                    PTloc = {}
                    groups = []
                    curg, used = [], 0
                    for kb in range(16):
                        ncols = 256 if (kb % nbr) < nbr - 1 else 128
                        if used + ncols > 512:
                            groups.append(curg)
                            curg, used = [], 0
                        curg.append((kb, used, ncols))
                        used += ncols
                    groups.append(curg)
                    ob, OB = [None], [None]

                    def do_pv(qb):
                        slot = qb % 4
                        if slot == 0:
                            ob[0], OB[0] = nb()
                        b_in = qb % nbr
                        pt, RP, off = PTloc[qb]
                        mm(ob[0][0:65, slot * 128:(slot + 1) * 128], vaug[:, qb, hh, 0:65], pt[:, off:off + 128], True, b_in == 0,
                           [VA[qb], RP], [OB[0]])
                        if b_in > 0:
                            pt2, RP2, off2 = PTloc[qb - 1]
                            mm(ob[0][0:65, slot * 128:(slot + 1) * 128], vaug[:, qb - 1, hh, 0:65], pt2[:, off2 + 128:off2 + 256], False, True,
                               [VA[qb - 1], RP2], [OB[0]])
                        if slot == 3:
                            pi0 = (qb - 3) * 128
                            if d == 1:
                                dst = oacc[0:65, hh, pi0:pi0 + 512]
                                src = ob[0][0:65, :]
                            elif d == 4:
                                dst = oacc[0:65, hh, 0:S].rearrange("p (n r) -> p r n", r=d)[:, pi0 // nper, :]
                                src = ob[0][0:65, :]
                            else:
                                r0 = pi0 // nper
                                dst = oacc[0:65, hh, 0:S].rearrange("p (n r) -> p r n", r=d)[:, r0:r0 + 4, :]
                                src = ob[0][0:65, :].rearrange("p (a b) -> p a b", a=4)
                            if g == 0:
                                cp("dve", dst, src, [OB[0]], [OA[hh]])
                            else:
                                tt_op("dve", dst, dst, src, ALU.add, [OB[0], OA[hh]], [OA[hh]])

                    def emit_S(grp):
                        sb_, SB_ = nb()
                        pi_ = pti[0] % 4
                        pti[0] += 1
                        for kb, off, ncols in grp:
                            r_ = kb // nbr
                            n0_ = (kb % nbr) * 128
                            lo_ = r_ + n0_ * d
                            mm(sb_[:, off:off + ncols], kT[pb:pb + 64, lo_:lo_ + 127 * d + 1:d], qT[pb:pb + 64, lo_:lo_ + (ncols - 1) * d + 1:d],
                               True, True, [R_k, R_q], [SB_])
                        tot = grp[-1][1] + grp[-1][2]
                        nc0 = grp[0][2]
                        if len(grp) > 1 and all(x[2] == nc0 for x in grp):
                            nk = len(grp)
                            tt_op("dve", ssb[pi_ % 2][:, 0:tot].rearrange("p (a b) -> p a b", a=nk),
                                  sb_[:, 0:tot].rearrange("p (a b) -> p a b", a=nk),
                                  biasp[:, hh * 256:hh * 256 + nc0].unsqueeze(1).broadcast_to([128, nk, nc0]), ALU.add,
                                  [SB_, R_bp], [SSB[pi_ % 2]])
                        else:
                            for kb, off, ncols in grp:
                                tt_op("dve", ssb[pi_ % 2][:, off:off + ncols], sb_[:, off:off + ncols], biasp[:, hh * 256:hh * 256 + ncols], ALU.add,
                                      [SB_, R_bp], [SSB[pi_ % 2]])
                        act(ptb[pi_][:, 0:tot], ssb[pi_ % 2][:, 0:tot], AF.Exp, [SSB[pi_ % 2]], [PTB[pi_]])
                        for kb, off, ncols in grp:
                            PTloc[kb] = (ptb[pi_], PTB[pi_], off)

                    done_pv = 0
                    prev_last = -1
                    for grp in groups:
                        emit_S(grp)
                        while done_pv <= prev_last:
                            do_pv(done_pv)
                            done_pv += 1
                        prev_last = grp[-1][0]
                        yield
                    while done_pv <= prev_last:
                        do_pv(done_pv)
                        done_pv += 1
                    yield

            def finalize(c):
                for hh in range(2):
                    for t in range(4):
                        t0 = t * 512
                        act(rl[64:65, t0:t0 + 512], oacc[64:65, hh, t0:t0 + 512], AF.Ln, [OA[hh]], [RL4[t]])
                        act(rl[64:65, t0:t0 + 512], rl[64:65, t0:t0 + 512], AF.Exp, [RL4[t]], [RL4[t]], scale=-1.0)
                    for t in range(4):
                        t0 = t * 512
                        bk, BK = nb()
                        mm(bk[0:64, :], ones_f[64:65, 0:64], rl[64:65, t0:t0 + 512], True, True, [R_ones, RL4[t]], [BK])
                        tt_op("dve", attT[hh * 64:(hh + 1) * 64, c, t0:t0 + 512], oacc[0:64, hh, t0:t0 + 512], bk[0:64, :], ALU.mult,
                              [OA[hh], BK], [AT[c][t_] for t_ in tov(t0, t0 + 512)])

            stages = [(c, g) for c in range(2) for g in range(3)]
            for _ in gen_proj(stages[0][0], stages[0][1], BS[0], pre0):
                pass
            PRATIO = int(os.environ.get("MK_PRATIO", "2"))
            for i_, (c, g) in enumerate(stages):
                A_ = gen_attn(c, g, BS[i_ % 2])
                P_ = gen_proj(stages[i_ + 1][0], stages[i_ + 1][1], BS[(i_ + 1) % 2]) if i_ + 1 < len(stages) else iter(())
                a_done = p_done = False
                while not (a_done and p_done):
                    if not a_done:
                        try:
                            next(A_)
                        except StopIteration:
                            a_done = True
                    for _ in range(PRATIO):
                        if not p_done:
                            try:
                                next(P_)
                            except StopIteration:
                                p_done = True
                if g == 2:
                    finalize(c)

            if sub < 4:
                return
            cur[0] = o_bp
            o_sel = take(1024)
            sel = bv(o_sel, 2048)
            R_sel = Region("sel")
            o_self = take(2048)
            self32 = fv(o_self, 2048)
            R_self = Region("self32")
            o_kc = take(4 * 2048)
            kcs = [fv(o_kc + 2048 * i, 2048).rearrange("p (j f) -> p j f", j=4) for i in range(4)]
            KCS = [Region("kc%d" % i) for i in range(4)]
            o_pr = take(1024)
            prod = fv(o_pr, 1024)
            o_s16 = take(16)
            s16 = fv(o_s16, 16)
            o_ptc = take(16)
            ptc = fv(o_ptc, 16)
            o_ptn = take(64)
            ptn = fv(o_ptn, 64)
            o_rls = take(64)
            rls = fv(o_rls, 64)
            R_prod, R_s16, R_ptc, R_ptn, R_rls = (Region(n) for n in ("prod", "s16", "ptc", "ptn", "rls"))
            smp_regs = KCS + [R_prod, R_s16, R_ptc, R_ptn, R_rls, R_sel, R_self]
            alias(smp_regs, phase_regs[len(persist):])
            del phase_regs[len(persist):]
            phase_regs.extend(smp_regs)
            dma("sp", self32[0:16, :], consts_d[0:16, C_SEL:C_SEL + 2048], R_self, writes=[R_self])
            cp("act", sel[0:16, :], self32[0:16, :], [R_self], [R_sel])
            biasc = cst[:, C_BIASC:C_BIASC + 48]
            biasn = cst[0:16, C_BIASN:C_BIASN + 192]
            ss_ = float(os.environ.get("MK_SS", "99"))
            ba, BA = nb(reserve=True)
            P.op("dve", lambda e: e.memset(ba[0:64, 0:128], 0.0), writes=[BA])
            kci = 0
            o_pr2 = take(1024)
            prods = [prod, fv(o_pr2, 1024)]
            o_pc2 = take(16)
            ptcs = [ptc, fv(o_pc2, 16)]
            o_s2 = take(16)
            s16s = [s16, fv(o_s2, 16)]
            R_prods = [R_prod, Region("prod2")]
            R_ptcs = [R_ptc, Region("ptc2")]
            R_s16s = [R_s16, Region("s16b")]
            extra = [R_prods[1], R_ptcs[1], R_s16s[1]]
            alias(extra, att_regs)
            phase_regs.extend(extra)

            def front(i, g, kci):
                d = DIL[g]
                J = 1 if g == 0 else 4
                kc, RKC = kcs[kci % 4], KCS[kci % 4]
                prod_, R_prod_ = prods[kci % 2], R_prods[kci % 2]
                s16_, R_s16_ = s16s[kci % 2], R_s16s[kci % 2]
                ptc_, R_ptc_ = ptcs[kci % 2], R_ptcs[kci % 2]
                if g == 0:
                    dma("sp", kc[:, 0, :], caches[0][l, i], RKC, writes=[RKC])
                else:
                    dma("sp", kc, caches[g][l, i].rearrange("(n r) f -> n r f", r=d)[:, 0:4, :], RKC, writes=[RKC])
                qb4 = [nb() for _ in range(4)]
                for j in range(4):
                    qb_, QB_ = qb4[j]
                    tok = 4 * i + j
                    mm(qb_[:, 0:256], sel[0:16, tok * 128:(tok + 1) * 128],
                       qtok[0:16, g, :, :].rearrange("p c f -> p (c f)"), True, True, [R_sel, R_qtok], [QB_])
                for j in range(4):
                    qb_, QB_ = qb4[j]
                    tt_op("dve", prod_[:, j * 256:(j + 1) * 256], kc[:, j if J == 4 else 0, 0:256], qb_[:, 0:256],
                          ALU.mult, [RKC, QB_], [R_prod_])
                P.op("dve", lambda e: e.tensor_reduce(out=s16_, in_=prod_.rearrange("p (a e) -> p a e", e=64), axis=AX.X, op=ALU.add),
                     [R_prod_], [R_s16_])
                tt_op("dve", s16_, s16_, biasc[:, g * 16:(g + 1) * 16], ALU.add, [R_s16_, R_cst], [R_s16_])
                act(ptc_, s16_, AF.Exp, [R_s16_], [R_ptc_])

            def back(i, g, kci):
                J = 1 if g == 0 else 4
                kc, RKC = kcs[kci % 4], KCS[kci % 4]
                ptc_, R_ptc_ = ptcs[kci % 2], R_ptcs[kci % 2]
                mm(ba[0:64, 64 + i * 16:64 + (i + 1) * 16], ones_f[:, 0:64], ptc_, False, False, [R_ones, R_ptc_], [BA])
                if g == 0:
                    for h in range(4):
                        mm(ba[0:64, i * 16 + h:i * 16 + 16:4], kc[:, 0, 256 + h * 64:256 + (h + 1) * 64], ptc_[:, h:16:4], False, False,
                           [RKC, R_ptc_], [BA])
                else:
                    for j in range(4):
                        for h in range(4):
                            col = i * 16 + j * 4 + h
                            mm(ba[0:64, col:col + 1], kc[:, j, 256 + h * 64:256 + (h + 1) * 64], ptc_[:, j * 4 + h:j * 4 + h + 1], False, False,
                               [RKC, R_ptc_], [BA])

            steps = [(i, g) for i in range(4) for g in range(3)]
            for n_, (i, g) in enumerate(steps):
                front(i, g, n_)
                if n_ > 0:
                    back(steps[n_ - 1][0], steps[n_ - 1][1], n_ - 1)
            back(steps[-1][0], steps[-1][1], len(steps) - 1)
            for g in range(3 if ss_ >= 3 else 0):
                bn, BN = nb()
                for h in range(4):
                    ch = g * 2 + h // 2
                    mm(bn[0:16, h * 16:(h + 1) * 16], kTs[:, ch, :], qTs[:, ch, h % 2, :], True, True, [R_kTs, R_qTs], [BN])
                tt_op("dve", ptn[0:16, :], bn[0:16, 0:64], biasn[:, g * 64:(g + 1) * 64], ALU.add, [BN, R_cst], [R_ptn])
                act(ptn[0:16, :], ptn[0:16, :], AF.Exp, [R_ptn], [R_ptn])
                for h in range(4):
                    mm(ba[0:64, 64 + h:128:4], ones_f[0:16, 0:64], ptn[0:16, h * 16:(h + 1) * 16], False, False, [R_ones, R_ptn], [BA])
                    mm(ba[0:64, h:64:4], kv_s[0:16, g, 1, h * 64:(h + 1) * 64], ptn[0:16, h * 16:(h + 1) * 16], False, g == 2 and h == 3,
                       [R_kvs, R_ptn], [BA])
            reserved.clear()
            if ss_ < 4:
                return
            P.op("dve", lambda e: e.reciprocal(out=rls[0:64, :], in_=ba[0:64, 64:128]), [BA], [R_rls])
            for h in range(4):
                tt_op("dve", attT[(h % 2) * 64:(h % 2) * 64 + 64, h // 2, S:NT], ba[0:64, h:64:4], rls[0:64, h:64:4], ALU.mult,
                      [BA, R_rls], [AT[h // 2][4]])
            for g in range(3):
                dma("sp", kvs[g, l], kv_s[0:16, g, :, :].rearrange("p kv f -> p (kv f)"), R_kvs, reads=[R_kvs])

            if sub < 5:
                return
            cur[0] = mixer_base
            o_sp = take(4 * NT // 2)
            spT = bv(o_sp, 4 * NT).rearrange("p (c n) -> p c n", c=4)
            SPT = [[Region("sp%d_%d" % (c, t)) for t in range(5)] for c in range(4)]
            o_vn = take(512)
            vnorm_bc = fv(o_vn, 512)
            o_gb = take(512)
            gbias_bc = fv(o_gb, 512)
            R_vn, R_gb = Region("vn"), Region("gb")
            o_vt = take(17 * 512 // 2)
            vst = bv(o_vt, 17 * 512).rearrange("p (b f) -> p b f", b=17)
            VS = [Region("vs%d" % b) for b in range(17)]
            o_u = take(2 * NT // 2)
            uT = [bv(o_u + (NT // 2) * i, NT) for i in range(2)]
            UT = [Region("u%d" % i) for i in range(2)]
            o_gl = take(2 * 512)
            gl = [fv(o_gl + 512 * i, 512) for i in range(2)]
            GL = [Region("gl%d" % i) for i in range(2)]
            o_junk = take(512)
            junk = fv(o_junk, 512)
            R_junk = Region("junk")
            spt_flat = [r for row in SPT for r in row]
            g_regs = spt_flat + VS + UT + GL + [R_junk, R_vn, R_gb]
            alias(g_regs, phase_regs[len(persist):])
            del phase_regs[len(persist):]
            phase_regs.extend(g_regs)
            dma("sp", vnorm_bc, vnorm_d[l:l + 1, :].broadcast_to([128, 512]), R_vn, writes=[R_vn])
            dma("sp", gbias_bc, gb_d[l:l + 1, :].broadcast_to([128, 512]), R_gb, writes=[R_gb])
            o_wsl = o_gl
            wsl = fv(o_gl, 512).rearrange("p (g s) -> p g s", g=4)
            dma("sp", wsl, ws_d[l].rearrange("g t s -> t g s"), GL[0], writes=[GL[0]])
            bk, BK = nb()
            for gq in range(4):
                P.op("pe", lambda e, gq=gq, bk=bk: e.transpose(bk[:, gq * 128:(gq + 1) * 128], wsl[:, gq, :], ident), [GL[0], R_cst], [BK])
            tt_op("dve", wsT, bk[:, :].rearrange("p (g t) -> p g t", g=4), triu.unsqueeze(1).broadcast_to([128, 4, 128]), ALU.mult,
                  [BK, R_cst], [R_wsT])
            P.op("dve", lambda e: e.memset(wsSf, 0.0), writes=[R_wsSf])
            for i in range(4):
                for gq in range(4):
                    dma("sp", wsSf[4 * i:4 * i + 4, gq * 16 + 4 * i:gq * 16 + 4 * i + 4],
                        ws_d[l][gq, 0:4, 0:4].rearrange("t s -> s t"), R_wsSf, writes=[R_wsSf], slow=True)
            tt_op("dve", wsS[0:16, :, :], wsSf[0:16, :].rearrange("p (g t) -> p g t", g=4),
                  cst[0:16, C_MASKS:C_MASKS + 64].rearrange("p (g t) -> p g t", g=4), ALU.mult, [R_wsSf, R_cst], [R_wsS])
            v2 = lambda b: b.rearrange("p (k n) -> p k n", k=KC)
            wv = []
            for s_ in range(2):
                w_, RW_ = wload(v2, w_in[l][:, 2816 + s_ * 256:2816 + (s_ + 1) * 256].rearrange("(k p) n -> p k n", p=128))
                wv.append((v2(w_), RW_))
            ssq = small[:, 0:17]
            P.op("dve", lambda e: e.memset(ssq, 1.0), writes=[R_small])
            for blk in range(17):
                np_ = 128 if blk < 16 else 16
                lo = blk * 128
                hreg = None
                bk, BK = nb()
                for s_ in range(2):
                    for k in range(KC):
                        mm(bk[0:np_, s_ * 256:(s_ + 1) * 256], hT[:, k, lo:lo + np_], wv[s_][0][:, k, :], k == 0, k == KC - 1, [wv[s_][1]] + [H[t_] for t_ in tov(lo, lo + np_)], [BK])
                gi_ = blk % 2
                if blk < 16:
                    act(gl[gi_][0:np_, :], bk[0:np_, :], AF.Gelu_apprx_tanh, [BK], [GL[gi_]])
                    act(junk[0:np_, :], gl[gi_][0:np_, :], AF.Square, [GL[gi_]], [R_junk, R_small], accum=small[0:np_, blk:blk + 1])
                    tt_op("dve", vst[:, blk, :], gl[gi_], vnorm_bc, ALU.mult, [GL[gi_], R_vn], [VS[blk]])
                else:
                    act(vs_s[0:16, :], bk[0:16, :], AF.Gelu_apprx_tanh, [BK], [R_vss])
                    act(junk[0:16, :], vs_s[0:16, :], AF.Square, [R_vss], [R_junk, R_small], accum=small[0:16, 16:17])
            act(ssq, ssq, AF.Ln, [R_small], [R_small], scale=1.0 / 512, bias=EPS)
            act(ssq, ssq, AF.Exp, [R_small], [R_small], scale=-0.5)
            for blk in range(16):
                ts_op("dve", vst[:, blk, :], vst[:, blk, :], small[:, blk:blk + 1], ALU.mult, [VS[blk], R_small], [VS[blk]])
            stt(vs_s[0:16, :], vs_s[0:16, :], small[0:16, 16:17], vnorm_bc[0:16, :], ALU.mult, ALU.mult, [R_vss, R_small, R_vn], [R_vss])
            cp("act", vst[0:16, 16, :], vs_s[0:16, :], [R_vss], [VS[16]])
            dma("sp", gvs[l], vs_s[0:16, :], R_vss, reads=[R_vss])
            v1 = lambda b: b[:, 0:1024].rearrange("p (k n) -> p k n", k=KC)

            def uproj(gq):
                wu_, RWU = wload(v1, w_in[l][:, 2304 + gq * 128:2304 + (gq + 1) * 128].rearrange("(k p) n -> p k n", p=128))
                wu3 = v1(wu_)
                ui = gq % 2
                for t in range(5):
                    t0, n = TT[t]
                    bk, BK = nb()
                    for k in range(KC):
                        mm(bk[:, 0:n], wu3[:, k, :], hT[:, k, t0:t0 + n], k == 0, k == KC - 1, [RWU, H[t]], [BK])
                    act(uT[ui][:, t0:t0 + n], bk[:, 0:n], AF.Gelu_apprx_tanh, [BK], [UT[ui]])

            def umix(gq):
                ui = gq % 2
                for t in range(5):
                    t0, n = LT[t]
                    bk, BK = nb()
                    if t < 4:
                        for j in range(4):
                            blk = 4 * t + j
                            mm(bk[:, j * 128:(j + 1) * 128], vst[:, blk, gq * 128:(gq + 1) * 128], wsT[:, gq, :], True, False, [VS[blk], R_wsT], [BK])
                            mm(bk[:, j * 128:(j + 1) * 128], ones_f[0:1, 0:128], gbias_bc[0:1, gq * 128:(gq + 1) * 128], False, True, [R_ones, R_gb], [BK])
                    else:
                        mm(bk[:, 0:16], vst[0:16, 16, gq * 128:(gq + 1) * 128], wsS[0:16, gq, :], True, False, [VS[16], R_wsS], [BK])
                        for i in range(4):
                            mm(bk[:, 4 * i:4 * i + 4], ones_f[0:1, 0:128], gbias_bc[0:1, gq * 128:gq * 128 + 4], False, i == 3, [R_ones, R_gb], [BK])
                    tt_op("dve", spT[:, gq, t0:t0 + n], uT[ui][:, t0:t0 + n], bk[:, 0:n], ALU.mult, [UT[ui], BK], [SPT[gq][t_] for t_ in tov(t0, t0 + n)])

            uproj(0)
            for gq in range(4):
                if gq + 1 < 4:
                    uproj(gq + 1)
                umix(gq)

            if sub < 6:
                return
            cur[0] = o_vn
            o_mx = take(KC * NT // 2)
            mxT = bv(o_mx, KC * NT).rearrange("p (k n) -> p k n", k=KC)
            MX = [[Region("mx%d_%d" % (k, t)) for t in range(5)] for k in range(KC)]
            o_ta = take(4 * 512)
            tmpA = [fv(o_ta + 512 * i, 512) for i in range(4)]
            TA = [Region("ta%d" % i) for i in range(4)]
            d_regs = [r for row in MX for r in row] + TA
            alias(d_regs, VS + UT + GL + [R_junk, R_vn, R_gb])
            del phase_regs[len(persist):]
            phase_regs.extend(spt_flat + d_regs)
            tai = 0
            for m2 in range(4):
                v2 = lambda b: b.rearrange("p (k n) -> p k n", k=KC)
                wga, RGA = wload(v2, w_in[l][:, 3328 + m2 * 256:3328 + (m2 + 1) * 256].rearrange("(k p) n -> p k n", p=128))
                wgb, RGB = wload(v2, w_in[l][:, 4352 + m2 * 256:4352 + (m2 + 1) * 256].rearrange("(k p) n -> p k n", p=128))
                vpa = lambda b: b[:, 0:512].rearrange("p (c n) -> p c n", c=2)
                vps = lambda b: b[:, 512:1536].rearrange("p (c n) -> p c n", c=4)
                wpa, RPA = wload(vpa, patt_d[l][:, m2 * 256:(m2 + 1) * 256].rearrange("(c p) n -> p c n", p=128))
                wps, RPS = wpa, RPA
                dma("pool", vps(wps), psp_d[l][:, m2 * 256:(m2 + 1) * 256].rearrange("(c p) n -> p c n", p=128), RPS, writes=[RPS])
                wga3, wgb3, wpa3, wps3 = v2(wga), v2(wgb), vpa(wpa), vps(wps)
                for mm_ in range(2):
                    m = m2 * 2 + mm_
                    ms = slice(mm_ * 128, (mm_ + 1) * 128)
                    for t in range(5):
                        t0, n = TT[t]
                        b1, B1 = nb()
                        for k in range(KC):
                            mm(b1[:, 0:n], wga3[:, k, ms], hT[:, k, t0:t0 + n], k == 0, k == KC - 1, [RGA, H[t]], [B1])
                        b2, B2 = nb()
                        for c in range(2):
                            mm(b2[:, 0:n], wpa3[:, c, ms], attT[:, c, t0:t0 + n], c == 0, c == 1, [RPA, AT[c][t]], [B2])
                        b3, B3 = nb()
                        for k in range(KC):
                            mm(b3[:, 0:n], wgb3[:, k, ms], hT[:, k, t0:t0 + n], k == 0, k == KC - 1, [RGB, H[t]], [B3])
                        b4, B4 = nb()
                        for c in range(4):
                            mm(b4[:, 0:n], wps3[:, c, ms], spT[:, c, t0:t0 + n], c == 0, c == 3, [RPS, SPT[c][t]], [B4])
                        ia, ib = tai % 4, (tai + 1) % 4
                        tai += 2
                        act(tmpA[ia][:, 0:n], b1[:, 0:n], AF.Sigmoid, [B1], [TA[ia]])
                        act(tmpA[ib][:, 0:n], b3[:, 0:n], AF.Sigmoid, [B3], [TA[ib]])
                        tt_op("dve", tmpA[ia][:, 0:n], tmpA[ia][:, 0:n], b2[:, 0:n], ALU.mult, [TA[ia], B2], [TA[ia]])
                        tt_op("dve", tmpA[ib][:, 0:n], tmpA[ib][:, 0:n], b4[:, 0:n], ALU.mult, [TA[ib], B4], [TA[ib]])
                        tt_op("dve", mxT[:, m, t0:t0 + n], tmpA[ia][:, 0:n], tmpA[ib][:, 0:n], ALU.add, [TA[ia], TA[ib]], [MX[m][t]])
            for m2 in range(4):
                v2 = lambda b: b.rearrange("p (k n) -> p k n", k=KC)
                wo, RO = wload(v2, wout_d[l][:, m2 * 256:(m2 + 1) * 256].rearrange("(k p) n -> p k n", p=128))
                wo3 = v2(wo)
                for mm_ in range(2):
                    m = m2 * 2 + mm_
                    for t in range(5):
                        t0, n = TT[t]
                        bk, BK = nb()
                        for k in range(KC):
                            mm(bk[:, 0:n], wo3[:, k, mm_ * 128:(mm_ + 1) * 128], mxT[:, k, t0:t0 + n], k == 0, k == KC - 1, [RO, MX[k][t]], [BK])
                        tt_op("dve", xT[:, m, t0:t0 + n], xT[:, m, t0:t0 + n], bk[:, 0:n], ALU.add, [BK, X[m][t]], [X[m][t]])

        def final():
            cur[0] = arena_base
            o_xt = take(2 * 1024)
            xtok = [fv(o_xt + 1024 * i, 1024) for i in range(2)]
            XTK = [Region("xtok%d" % i) for i in range(2)]
            o_yt = take(2 * 1024)
            ytok = [fv(o_yt + 1024 * i, 1024) for i in range(2)]
            YTK = [Region("ytok%d" % i) for i in range(2)]
            o_fg = take(1024)
            fg_bc = fv(o_fg, 1024)
            R_fg = Region("fg")
            o_jk = take(1024)
            junk = fv(o_jk, 1024)
            R_junk = Region("junkf")
            regs = XTK + YTK + [R_fg, R_junk]
            new_phase(regs)
            OUTR.extend(YTK)
            dma("sp", fg_bc, fnorm_d.broadcast_to([128, 1024]), R_fg, writes=[R_fg])
            for blk in range(17):
                np_ = 128 if blk < 16 else 16
                lo = blk * 128
                i = blk % 2
                b0, B0 = nb()
                b1, B1 = nb()
                for k in range(KC):
                    bk, BK = (b0, B0) if k < 4 else (b1, B1)
                    P.op("pe", lambda e, bk=bk, k=k, lo=lo, np_=np_: e.transpose(bk[0:np_, (k % 4) * 128:(k % 4 + 1) * 128], xT[:, k, lo:lo + np_], ident),
                         [X[k][t_] for t_ in tov(lo, lo + np_)] + [R_cst], [BK])
                cp("act", xtok[i][0:np_, 0:512], b0[0:np_, :], [B0], [XTK[i]])
                cp("dve", xtok[i][0:np_, 512:1024], b1[0:np_, :], [B1], [XTK[i]])
                ssc = small[0:np_, 32 + (blk % 8) * 2:32 + (blk % 8) * 2 + 1]
                act(junk[0:np_, :], xtok[i][0:np_, :], AF.Square, [XTK[i]], [R_junk, R_small], accum=ssc)
                ts_op("dve", ssc, ssc, 1.0 / D, ALU.mult, [R_small], [R_small], s2=EPS, op1=ALU.add)
                act(ssc, ssc, AF.Sqrt, [R_small], [R_small])
                P.op("dve", lambda e, ssc=ssc: e.reciprocal(out=ssc, in_=ssc), [R_small], [R_small])
                stt(ytok[i][0:np_, :], xtok[i][0:np_, :], ssc, fg_bc[0:np_, :], ALU.mult, ALU.mult, [XTK[i], R_small, R_fg], [YTK[i]])
                if blk < 16:
                    dma("sp", yp[lo:lo + 128, :], ytok[i], YTK[i], reads=[YTK[i]])
                else:
                    dma("sp", ys, ytok[i][0:16, :], YTK[i], reads=[YTK[i]])
            return regs

        fnorm_d = din("final_norm", [1, D])

        stage = int(os.environ.get("MK_STAGE", "99"))
        nstep = 0
        for l in range(DEPTH):
            for fn_, a_ in ((ffn, (l, 0)), (mixer, (l,)), (ffn, (l, 1))):
                if nstep < stage:
                    fn_(*a_)
                nstep += 1
        final()
        P.finish("sp", list(phase_regs) + OUTR)
        stats = P.emit(st)
        print("planner stats", stats)
    return nc


_CACHE = {}


def kernel(**inputs):
    f32 = lambda a: np.ascontiguousarray(np.asarray(a, dtype=np.float32))
    if "nc" not in _CACHE:
        _CACHE["nc"] = build_program()
        _CACHE["consts"] = build_consts()
    nc = _CACHE["nc"]
    consts = _CACHE["consts"]
    gl = []
    for l in range(DEPTH):
        for nm in ("ffn1_norm", "mix_norm", "ffn2_norm"):
            gl.append(f32(inputs[nm])[l])
    gl.append(f32(inputs["final_norm"]))
    gains = np.zeros((128, 56), np.float32)
    for i, v in enumerate(gl):
        gains[:, i * 8:(i + 1) * 8] = v.reshape(8, 128).T
    shared = {
        "gains": gains, "consts": consts,
        "gmlp_v_norm": f32(inputs["gmlp_v_norm"]),
        "gmlp_ws": f32(inputs["gmlp_ws"]),
        "gmlp_bias": f32(inputs["gmlp_bias"]).reshape(DEPTH, 512),
        "proj_att": f32(inputs["proj_att"]), "proj_spatial": f32(inputs["proj_spatial"]),
        "w_out": f32(inputs["w_out"]), "w_in": f32(inputs["w_in"]),
        "final_norm": f32(inputs["final_norm"]).reshape(1, D),
    }
    for nm in ("ffn1_gate", "ffn1_up", "ffn1_down", "ffn2_gate", "ffn2_up", "ffn2_down"):
        shared[nm] = f32(inputs[nm])
    xp = f32(inputs["x_prompt"])
    xs = f32(inputs["x_sample"])
    cks = [f32(inputs["cache_kv_w128"]), f32(inputs["cache_kv_w512"]), f32(inputs["cache_kv_w2048"])]
    in_maps = []
    for c in range(NCORES):
        m = dict(shared)
        m["xp"] = xp[c]
        m["xs"] = xs[4 * c:4 * c + 4].reshape(NS, D)
        for w, ck in zip(WIN, cks):
            m["c%d" % w] = np.ascontiguousarray(ck[:, 4 * c:4 * c + 4].reshape(DEPTH, 4, w, 512))
        in_maps.append(m)
    res = run_bass_kernel_spmd(nc, in_maps, core_ids=list(range(NCORES)))
    R = res.results
    y_prompt = np.stack([R[c]["yp"] for c in range(NCORES)], 0)
    y_sample = np.concatenate([R[c]["ys"].reshape(4, 4, D) for c in range(NCORES)], 0)
    outs = [y_prompt, y_sample]
    for gi, w in enumerate(WIN):
        a = np.stack([R[c]["kvp%d" % w] for c in range(NCORES)], 1)
        outs.append(a.reshape(DEPTH, NCORES, w, 2, 4, 64))
    for gi in range(3):
        a = np.concatenate([R[c]["kvs"][gi].reshape(DEPTH, 4, 4, 512) for c in range(NCORES)], 1)
        outs.append(a.reshape(DEPTH, 32, 4, 2, 4, 64))
    gv = np.concatenate([R[c]["gvs"].reshape(DEPTH, 4, 4, 512) for c in range(NCORES)], 1)
    outs.append(gv)
    return tuple(np.ascontiguousarray(o, dtype=np.float32) for o in outs)
```
